# Optimizing a Trainium2 kernel written in Bass

```python
import jax, jax.numpy as jnp
from jax import lax
import numpy as np

D_MODEL = 1024
BATCH = 8
SEQ = 2048
DEPTH = 1
DEC_BATCH = 128
DEC_SEQ = 4
PAST_LEN = 16384
PAGE_SIZE = 128

D_MIX = D_MODEL
D_POOL = D_MIX // 2
D_CONV = D_MIX - D_POOL
POOL_WINDOWS = (2, 4, 8, 16)
N_POOL_GROUPS = len(POOL_WINDOWS)
POOL_GC = D_POOL // N_POOL_GROUPS
POOL_BUF = max(POOL_WINDOWS) - 1
N_CONV_HEADS = 8
CONV_HEAD_DIM = D_CONV // N_CONV_HEADS
CONV_W = 3
D_FF = 2816
D_IN_PROJ = D_POOL + 3 * D_CONV
RMS_EPS = 1e-6

kernel_name = 'hybrid_pool_shortconv_convffn_step'


def rmsnorm(x, g):
    xf = x.astype(jnp.float32)
    r = xf * lax.rsqrt(jnp.mean(xf * xf, axis=-1, keepdims=True) + RMS_EPS)
    return (r * g.astype(jnp.float32)).astype(x.dtype)


def causal_dwconv(p, w):
    T = p.shape[1] - (CONV_W - 1)
    out = p[:, 0:T] * w[0]
    for k in range(1, CONV_W):
        out = out + p[:, k:k + T] * w[k]
    return out


def pool_mixer(p, pos0, pool_w, pool_scale):
    T = p.shape[1] - POOL_BUF
    pf = p.astype(jnp.float32)
    cs = jnp.concatenate([jnp.zeros_like(pf[:, :1]), jnp.cumsum(pf, axis=1)], axis=1)
    cur = pf[:, POOL_BUF:]
    t = jnp.arange(T, dtype=jnp.int32)
    outs = []
    for g, w in enumerate(POOL_WINDOWS):
        sl = slice(g * POOL_GC, (g + 1) * POOL_GC)
        s = cs[:, POOL_BUF + 1:POOL_BUF + 1 + T, sl] - cs[:, POOL_BUF + 1 - w:POOL_BUF + 1 - w + T, sl]
        cnt = jnp.minimum(w, pos0 + t + 1).astype(jnp.float32)[None, :, None]
        outs.append(s / cnt - cur[..., sl])
    d = jnp.stack(outs, axis=2).astype(p.dtype)
    y = jnp.einsum('btgc,gcd->btgd', d, pool_w)
    return y.reshape(y.shape[0], T, D_POOL) * pool_scale


def trunk_layer(x, c, pool_buf, conv_buf, ffn_buf, pos0, w_ada, b_ada, g_pre_mix, g_post_mix,
                g_pre_ffn, g_post_ffn, w_in, pool_w, pool_scale, conv_w, w_out,
                ffn_w_up, ffn_conv_w, ffn_w_down):
    mod = (jax.nn.silu(c) @ w_ada + b_ada)[:, None, :]
    sh1, sc1, gt1, sh2, sc2, gt2 = jnp.split(mod, 6, axis=-1)
    h = rmsnorm(x, g_pre_mix) * (1 + sc1) + sh1
    proj = h @ w_in
    v_pool, x_conv, gate_b, gate_c = jnp.split(
        proj, [D_POOL, D_POOL + D_CONV, D_POOL + 2 * D_CONV], axis=-1)
    pool_in = jnp.concatenate([pool_buf, v_pool], axis=1)
    y_pool = pool_mixer(pool_in, pos0, pool_w, pool_scale)
    conv_in = jnp.concatenate([conv_buf, gate_c * x_conv], axis=1)
    y_conv = gate_b * causal_dwconv(conv_in, conv_w)
    mix = jnp.concatenate([y_pool, y_conv], axis=-1) @ w_out
    x = x + gt1 * rmsnorm(mix, g_post_mix)
    h2 = rmsnorm(x, g_pre_ffn) * (1 + sc2) + sh2
    up = h2 @ ffn_w_up
    ffn_in = jnp.concatenate([ffn_buf, up], axis=1)
    a, b = jnp.split(causal_dwconv(ffn_in, ffn_conv_w), 2, axis=-1)
    f = (jax.nn.silu(a) * b) @ ffn_w_down
    x = x + gt2 * rmsnorm(f, g_post_ffn)
    return x, pool_in[:, -POOL_BUF:], conv_in[:, -(CONV_W - 1):], ffn_in[:, -(CONV_W - 1):]


def setup_inputs(seed: int = 0) -> dict:
    key = jax.random.key(seed)
    ks = jax.random.split(key, 24)
    f32 = jnp.float32
    nrm = lambda k, s, sc: jax.random.normal(k, s, f32) * sc
    return {
        'x_prompt': nrm(ks[0], (BATCH, SEQ, D_MODEL), 1.0),
        'x_sample': nrm(ks[1], (DEC_BATCH, DEC_SEQ, D_MODEL), 1.0),
        'state_pool': nrm(ks[2], (DEPTH, DEC_BATCH, POOL_BUF, D_POOL), 1.0),
        'state_conv': nrm(ks[3], (DEPTH, DEC_BATCH, CONV_W - 1, D_CONV), 1.0),
        'state_ffn': nrm(ks[4], (DEPTH, DEC_BATCH, CONV_W - 1, 2 * D_FF), 1.0),
        'c_prompt': nrm(ks[5], (BATCH, D_MODEL), 1.0),
        'c_sample': nrm(ks[6], (DEC_BATCH, D_MODEL), 1.0),
        'w_ada': nrm(ks[7], (DEPTH, D_MODEL, 6 * D_MODEL), 0.02),
        'b_ada': nrm(ks[8], (DEPTH, 6 * D_MODEL), 0.02),
        'g_pre_mix': 1.0 + nrm(ks[9], (DEPTH, D_MODEL), 0.05),
        'g_post_mix': 1.0 + nrm(ks[10], (DEPTH, D_MODEL), 0.05),
        'g_pre_ffn': 1.0 + nrm(ks[11], (DEPTH, D_MODEL), 0.05),
        'g_post_ffn': 1.0 + nrm(ks[12], (DEPTH, D_MODEL), 0.05),
        'w_in': nrm(ks[13], (DEPTH, D_MODEL, D_IN_PROJ), D_MODEL ** -0.5),
        'pool_w': nrm(ks[14], (DEPTH, N_POOL_GROUPS, POOL_GC, POOL_GC), POOL_GC ** -0.5),
        'pool_scale': 1.0 + nrm(ks[15], (DEPTH, D_POOL), 0.1),
        'conv_w': nrm(ks[16], (DEPTH, CONV_W, D_CONV), CONV_W ** -0.5),
        'w_out': nrm(ks[17], (DEPTH, D_MIX, D_MODEL), D_MIX ** -0.5),
        'ffn_w_up': nrm(ks[18], (DEPTH, D_MODEL, 2 * D_FF), D_MODEL ** -0.5),
        'ffn_conv_w': nrm(ks[19], (DEPTH, CONV_W, 2 * D_FF), CONV_W ** -0.5),
        'ffn_w_down': nrm(ks[20], (DEPTH, D_FF, D_MODEL), D_FF ** -0.5),
    }


def reference(x_prompt, x_sample, state_pool, state_conv, state_ffn, c_prompt, c_sample,
              w_ada, b_ada, g_pre_mix, g_post_mix, g_pre_ffn, g_post_ffn, w_in, pool_w,
              pool_scale, conv_w, w_out, ffn_w_up, ffn_conv_w, ffn_w_down):
    xp, xs = x_prompt, x_sample
    npp, ncp, nfp, nps, ncs, nfs = [], [], [], [], [], []
    for l in range(DEPTH):
        wl = (w_ada[l], b_ada[l], g_pre_mix[l], g_post_mix[l], g_pre_ffn[l], g_post_ffn[l],
              w_in[l], pool_w[l], pool_scale[l], conv_w[l], w_out[l],
              ffn_w_up[l], ffn_conv_w[l], ffn_w_down[l])
        zp = jnp.zeros((xp.shape[0], POOL_BUF, D_POOL), xp.dtype)
        zc = jnp.zeros((xp.shape[0], CONV_W - 1, D_CONV), xp.dtype)
        zf = jnp.zeros((xp.shape[0], CONV_W - 1, 2 * D_FF), xp.dtype)
        xp, sp, sc, sf = trunk_layer(xp, c_prompt, zp, zc, zf, 0, *wl)
        npp.append(sp); ncp.append(sc); nfp.append(sf)
        xs, sp2, sc2, sf2 = trunk_layer(xs, c_sample, state_pool[l], state_conv[l], state_ffn[l],
                                        PAST_LEN, *wl)
        nps.append(sp2); ncs.append(sc2); nfs.append(sf2)
    return (xp, xs, jnp.stack(npp), jnp.stack(ncp), jnp.stack(nfp),
            jnp.stack(nps), jnp.stack(ncs), jnp.stack(nfs))
```

```python
import numpy as np
from contextlib import ExitStack

import concourse.bass as bass
import concourse.mybir as mybir
from concourse.bass_utils import run_bass_kernel_spmd

F32 = mybir.dt.float32
BF16 = mybir.dt.bfloat16
AF = mybir.ActivationFunctionType
ALU = mybir.AluOpType

NCORES = 8
D = 1024
KC = 8
T = 2048
NS = 16
DS = 4
TS = NS * DS
NTOK = T + TS
DFF = 2816
NPAIR = 22
NCH = 44
EPS = 1e-6
WIN = (2, 4, 8, 16)
PB = 15
NV = 196

V_GPRE = 0
V_GFFN = 8
V_BADA = 16
V_PSC = 48
V_CW = 52
V_FW = 64


class Region:
    __slots__ = ("name", "w", "r")

    def __init__(self, name=""):
        self.name = name
        self.w = None
        self.r = []

    def alias(self, olds):
        for o in olds:
            if o.w is not None:
                self.r.append(o.w)
            self.r.extend(o.r)


class Sem:
    __slots__ = ("h", "count")

    def __init__(self, h):
        self.h = h
        self.count = 0


class Op:
    __slots__ = ("eng", "fn", "deps", "sig", "need", "dma", "sem", "n")


class Sched:
    ENGS = ("pe", "act", "dve", "pool", "sp")

    def __init__(self):
        self.ops = {e: [] for e in self.ENGS}
        self.all = []

    def _mk(self, eng, fn, reads, writes, deps):
        o = Op()
        o.eng = eng
        o.fn = fn
        o.sig = None
        o.need = False
        o.dma = False
        o.sem = None
        o.n = 0
        d = set(deps)
        for r in reads:
            if r.w is not None:
                d.add(r.w)
        for w in writes:
            if w.w is not None:
                d.add(w.w)
            d.update(w.r)
        for r in reads:
            r.r.append(o)
        for w in writes:
            w.w = o
            w.r = []
        d.discard(o)
        o.deps = d
        self.ops[eng].append(o)
        self.all.append(o)
        return o

    def op(self, eng, fn, reads=(), writes=(), deps=()):
        return self._mk(eng, fn, reads, writes, deps)

    def dma(self, eng, fn, sem, n, reads=(), writes=(), deps=()):
        o = self._mk(eng, fn, reads, writes, deps)
        o.dma = True
        o.sem = sem
        o.n = n
        sem.count += 16 * n
        o.sig = (sem, sem.count)
        return o

    def finalize(self, esems):
        for o in self.all:
            for d in o.deps:
                d.need = True
        cnt = {e: 0 for e in self.ENGS}
        for o in self.all:
            if o.need and not o.dma:
                cnt[o.eng] += 1
                o.sig = (esems[o.eng], cnt[o.eng])

    def emit(self, eng, e):
        waited = {}
        for o in self.ops[eng]:
            need = {}
            for d in o.deps:
                s, v = d.sig
                if need.get(s, 0) < v:
                    need[s] = v
            for s, v in need.items():
                if waited.get(s, 0) < v:
                    e.wait_ge(s.h, v)
                    waited[s] = v
            if o.fn is None:
                continue
            r = o.fn(e)
            if o.dma:
                assert len(r) == o.n, (len(r), o.n)
                for i in r:
                    i.then_inc(o.sem.h, 16)
            elif o.need:
                r.then_inc(o.sig[0].h, 1)


def chain(K, eng, fns, reads, writes):
    o = None
    for f in fns:
        o = K.op(eng, f, reads=reads, writes=writes)
    return o


def mm_group(out_ap, lhs_list, rhs_list):
    def fn(e):
        n = len(lhs_list)
        inst = None
        for i in range(n):
            inst = e.matmul(out_ap, lhs_list[i], rhs_list[i], start=(i == 0), stop=(i == n - 1))
        return inst
    return fn


def mm_part(out_ap, lhs_list, rhs_list, first, last):
    def fn(e):
        n = len(lhs_list)
        inst = None
        for i in range(n):
            inst = e.matmul(out_ap, lhs_list[i], rhs_list[i], start=(first and i == 0), stop=(last and i == n - 1))
        return inst
    return fn


def tr_group(outs, ins, ident):
    def fn(e):
        inst = None
        for o, i in zip(outs, ins):
            inst = e.transpose(o, i, ident)
        return inst
    return fn


STAGE = 99


def build_program():
    nc = bass.Bass("TRN2", target_bir_lowering=False)
    es = ExitStack()

    def din(name, shape):
        return nc.dram_tensor(name, list(shape), F32, kind="ExternalInput").ap()

    def dout(name, shape):
        return nc.dram_tensor(name, list(shape), F32, kind="ExternalOutput").ap()

    x_d = din("x", [NTOK, D])
    cT_d = din("cT", [D, 18])
    stp_d = din("st_pool", [NS * PB, 512])
    stc_d = din("st_conv", [NS * 2, 512])
    stf_d = din("st_ffn", [NS * 2, 2 * DFF])
    wada_d = din("w_ada", [D, 6 * D])
    win_d = din("w_in", [D, 2048])
    pw_d = din("pool_w", [512, 128])
    wout_d = din("w_out", [D, D])
    wup_d = din("w_up", [D, 2 * DFF])
    wdn_d = din("w_down", [DFF, D])
    vecs_d = din("vecs", [128, NV])
    rows_d = din("rows", [4, D])
    ident_d = din("ident", [128, 128])

    y_d = dout("y", [NTOK, D])
    npp_d = dout("npp", [PB, 512])
    ncp_d = dout("ncp", [2, 512])
    nfp_d = dout("nfp", [2, 2 * DFF])
    nps_d = dout("nps", [NS, PB, 512])
    ncs_d = dout("ncs", [NS * 2, 512])
    nfs_d = dout("nfs", [NS * 2, 2 * DFF])

    ARENA_BYTES = 206 * 1024
    arena = es.enter_context(nc.sbuf_tensor("arena", [128, ARENA_BYTES // 2], BF16))
    cur = [0]

    def alloc(shape, dtype, at=None):
        esz = 4 if dtype == F32 else 2
        n = 1
        for s in shape[1:]:
            n *= s
        nbytes = n * esz
        nbytes = (nbytes + 31) // 32 * 32
        off = cur[0] if at is None else at
        if at is None:
            cur[0] += nbytes
        assert off + nbytes <= ARENA_BYTES, ("SBUF arena overflow", off, nbytes)
        ap = arena[:, off // 2: off // 2 + n * esz // 2]
        if dtype == F32:
            ap = ap.bitcast(F32)
        if len(shape) == 3:
            ap = ap.rearrange("p (a b) -> p a b", b=shape[2])
        elif len(shape) == 4:
            ap = ap.rearrange("p (a b c) -> p a b c", b=shape[2], c=shape[3])
        if shape[0] < 128:
            ap = ap[0:shape[0]]
        return ap

    VECS = alloc([128, NV], F32)
    VD = alloc([128, 64], F32)
    CTS = alloc([128, KC, 18], F32)
    SC = alloc([128, KC, 18], BF16)
    REP = alloc([128, KC, 192], BF16)
    G1T = alloc([128, KC, 17], F32)
    S1T = alloc([128, KC, 17], F32)
    G2T = alloc([128, KC, 17], F32)
    S2T = alloc([128, KC, 17], F32)
    IDF = alloc([128, 128], F32)
    IDB = alloc([128, 128], BF16)
    STAT = alloc([128, 8, 20], F32)
    INVC = alloc([128, 4, 16], F32)
    HALO = alloc([128, NCH, 2], F32)
    HALOB = alloc([128, NCH, 2], F32)
    STF = alloc([128, NCH, 32], F32)
    GTP = alloc([128, D], F32)
    GTS = alloc([128, D], F32)
    EPSB = alloc([128, 1], F32)
    RING_N = 3
    RING = [alloc([128, KC, 512], BF16) for _ in range(RING_N)]
    NXT = 4
    XT = [alloc([128, D], F32) for _ in range(NXT)]
    TMP = alloc([128, D], F32)
    JUNK = alloc([128, D], BF16)
    phase_mark = cur[0]
    AH = alloc([128, KC, NTOK], BF16)
    BY = alloc([128, KC, NTOK], BF16)
    POOLW = alloc([128, 4, 128], BF16)
    TBW = 528
    NTB = 10
    TBUF = [alloc([128, TBW], F32) for _ in range(NTB)]
    DBUF = [alloc([128, 512], BF16) for _ in range(3)]
    STG = alloc([128, 512], F32)
    STG2 = alloc([128, 512], F32)
    STG1C = alloc([128, 512], F32)
    VS = alloc([128, 4, NS, PB + DS], F32)
    QS = alloc([128, 4, NS, 2 + DS], F32)
    MROW = alloc([128, 512], F32)
    SMP = alloc([128, 4, TS], F32)
    SMP2 = alloc([128, 2 * NS], F32)
    NXN1 = 5
    WO0 = alloc([128, KC, 512], BF16)
    GT1P = alloc([128, D], F32)
    GT1S = alloc([128, D], F32)
    XN1 = [alloc([128, D], BF16) for _ in range(NXN1)]
    p1_end = cur[0]
    cur[0] = phase_mark
    GCOLS = 1024 + TS
    H2 = alloc([128, KC, GCOLS], BF16)
    GT_ = alloc([128, NPAIR, GCOLS], BF16)
    WDN = alloc([128, NPAIR, D], BF16)
    NT2 = 3
    TB2 = [alloc([128, 512], F32) for _ in range(2 * NT2)]
    HBUF = alloc([128, 2, 2, 2], F32)
    HW1 = alloc([128, 2, 2, 2], F32)
    USB = [alloc([128, 4, NS, 6], F32) for _ in range(2)]
    TSB = [alloc([128, 4, TS], F32) for _ in range(2)]
    STG3 = alloc([128, 512], F32)
    XN2 = [alloc([128, D], BF16) for _ in range(2)]
    p2_end = cur[0]
    ROWB = TMP
    print("SBUF bytes/partition: shared %d, phase1 %d, phase2 %d" % (phase_mark, p1_end, p2_end))

    PSF = es.enter_context(nc.psum_tensor("psf", [128, 6, 512], F32))
    PSB = es.enter_context(nc.psum_tensor("psb", [128, 2, D], BF16))

    def new_sem(name):
        return Sem(es.enter_context(nc.semaphore(name)))

    esems = {e: new_sem("e_" + e) for e in Sched.ENGS}
    K = Sched()

    R = Region
    r_ring = [R("ring%d" % i) for i in range(RING_N)]
    s_ring = [new_sem("ring%d" % i) for i in range(RING_N)]
    r_psf = [R("psf%d" % i) for i in range(6)]
    r_psb = [R("psb%d" % i) for i in range(2)]
    r_xt = [R("xt%d" % i) for i in range(NXT)]
    s_xt = [new_sem("xt%d" % i) for i in range(NXT)]
    s_xtp = [new_sem("xtp%d" % i) for i in range(NXT)]
    r_xn1 = [R("xn1_%d" % i) for i in range(NXN1)]
    r_xn2 = [R("xn2_%d" % i) for i in range(2)]
    xn_i = [0]
    r_tmp = R("tmp")
    r_junk = R("junk")
    r_vecs = R("vecs")
    r_vd = R("vd")
    r_sc = R("sc")
    r_rep = R("rep")
    r_mod = R("mod")
    r_gt = R("gt")
    r_gt1 = R("gt1")
    r_wo0 = R("wo0")
    s_wo0 = new_sem("wo0")
    r_id = R("id")
    r_idb = R("idb")
    r_mrow = R("mrow")
    r_vs = R("vs")
    r_qs = R("qs")
    r_stf = R("stf")
    r_stg1c = R("stg1c")
    r_invc = R("invc")
    r_smp = [R("smp%d" % i) for i in range(4)]
    r_smp2 = R("smp2")
    s_setup = new_sem("setup")
    s_setup_p = new_sem("setup_p")
    s_out = new_sem("out")
    s_o_stg = new_sem("o_stg")
    s_o_stg2 = new_sem("o_stg2")
    s_o_stg3 = new_sem("o_stg3")
    s_o_tb = [new_sem("o_tb0"), new_sem("o_tb1")]
    s_o_cp = new_sem("o_cp")
    s_o_xt = [new_sem("o_xt%d" % i) for i in range(NXT)]
    out_ops = []

    ring_i = [0]
    psf_i = [0]
    psb_i = [0]
    xt_i = [0]
    misc_i = [0]

    def next_ring():
        i = ring_i[0] % RING_N
        ring_i[0] += 1
        return i

    psf_free = list(range(6))

    def next_psf():
        assert psf_free, "out of PSUM banks"
        return psf_free.pop(0)

    def next_psf_pair():
        for b0 in list(psf_free):
            if b0 % 2 == 0 and b0 + 1 in psf_free:
                psf_free.remove(b0)
                psf_free.remove(b0 + 1)
                return b0
        raise AssertionError("out of PSUM bank pairs")

    def rel_psf(*bs):
        for b_ in bs:
            assert b_ not in psf_free
            psf_free.append(b_)

    def next_psb():
        i = psb_i[0] % 2
        psb_i[0] += 1
        return i

    xt_free = list(range(NXT))

    def next_xt():
        assert xt_free, "out of XT slots"
        return xt_free.pop(0)

    def release_xt(i):
        assert i not in xt_free
        xt_free.append(i)

    def next_misc():
        misc_i[0] += 1
        return new_sem("misc%d" % misc_i[0])

    dbg = {}

    def dump(name, ap, reads):
        shp = list(ap.shape)
        dt = nc.dram_tensor("dbg_" + name, shp, ap.dtype, kind="ExternalOutput").ap()
        dbg[name] = shp
        out_ops.append(K.dma("sp", lambda e: [e.dma_start(out=dt, in_=ap)], s_out, 1, reads=reads))

    def finish():
        K.op("sp", None, deps=out_ops)
        fix_setup()
        K.finalize(esems)
        with nc.Block() as block:
            @block.tensor
            def _(e):
                K.emit("pe", e)

            @block.scalar
            def _(e):
                K.emit("act", e)

            @block.vector
            def _(e):
                K.emit("dve", e)

            @block.gpsimd
            def _(e):
                K.emit("pool", e)

            @block.sync
            def _(e):
                K.emit("sp", e)
        es.close()
        nc._dbg = dbg
        return nc

    def setup_dma(eng, out_ap, in_ap, writes):
        return K.dma(eng, lambda e, o=out_ap, i=in_ap: [e.dma_start(out=o, in_=i)],
                     s_setup_p if eng == "pool" else s_setup, 1, writes=writes)

    setup_ops = []
    setup_ops.append(setup_dma("sp", VECS, vecs_d, [r_vecs]))
    setup_ops.append(setup_dma("sp", CTS, cT_d.rearrange("(kc p) s -> p kc s", p=128), [r_sc]))
    setup_ops.append(setup_dma("sp", IDF, ident_d, [r_id]))
    setup_ops.append(setup_dma("pool", IDB, ident_d, [r_idb]))
    r_poolw = R("poolw")
    setup_ops.append(setup_dma("pool", POOLW, pw_d.rearrange("(g c) d -> c g d", c=128), [r_poolw]))

    def fix_setup():
        for o in setup_ops:
            o.sig = (o.sem, o.sem.count)

    K.op("dve", lambda e: e.memset(EPSB, EPS), writes=[r_vd])

    def _invc(e):
        inst = None
        for g, w in enumerate(WIN):
            for t in range(w - 1):
                inst = e.memset(INVC[:, g, t:t + 1], 1.0 / (t + 1))
            inst = e.memset(INVC[:, g, w - 1:16], 1.0 / w)
        return inst
    K.op("dve", _invc, writes=[r_invc])

    def _vd(e):
        e.tensor_scalar(VD[:, 0:8], VECS[:, V_BADA + 8:V_BADA + 16], 1.0, None, ALU.add)
        return e.tensor_scalar(VD[:, 8:16], VECS[:, V_BADA + 24:V_BADA + 32], 1.0, None, ALU.add)
    r_vd2 = R("vd2")
    K.op("dve", _vd, reads=[r_vecs], writes=[r_vd2])

    K.op("act", lambda e: e.activation(out=SC, in_=CTS, func=AF.Silu), reads=[r_sc], writes=[r_sc])

    def _rep(e):
        e.tensor_copy(REP[:, :, 0:128], SC[:, :, 0:1].to_broadcast([128, KC, 128]))
        return e.tensor_copy(REP[:, :, 128:192].rearrange("p k (s r) -> p k s r", r=DS),
                             SC[:, :, 1:17].unsqueeze(3).to_broadcast([128, KC, NS, DS]))
    K.op("dve", _rep, reads=[r_sc], writes=[r_rep])

    def load_rows(dst, src, rows, rg, eng="sp"):
        sem = next_misc()
        return K.dma(eng, lambda e: [e.dma_start(out=dst[0:rows, :], in_=src)], sem, 1, writes=[rg])

    r_stg = R("stg")
    r_stg2 = R("stg2")

    def state_loads_early():
        load_rows(STG, stp_d[0:120, :], 120, r_stg)
        load_rows(STG2, stp_d[120:240, :], 120, r_stg2)
        load_rows(STG1C, stc_d, 32, r_stg1c)

    def state_transposes_early():
        for h, (stg, rg) in enumerate(((STG, r_stg), (STG2, r_stg2))):
            b = next_psf()
            K.op("pe", tr_group([PSF[:, b, g * 120:(g + 1) * 120] for g in range(4)],
                                [stg[0:120, g * 128:(g + 1) * 128] for g in range(4)], IDF[0:120, 0:120]),
                 reads=[rg, r_id], writes=[r_psf[b]])
            K.op("act", lambda e, b=b, h=h: e.activation(
                out=VS[:, :, h * 8:(h + 1) * 8, 0:PB],
                in_=PSF[:, b, 0:480].rearrange("p (g s r) -> p g s r", g=4, r=PB), func=AF.Copy),
                reads=[r_psf[b]], writes=[r_vs])
            rel_psf(b)
        b = next_psf()
        K.op("pe", tr_group([PSF[:, b, j * 32:(j + 1) * 32] for j in range(4)],
                            [STG1C[0:32, j * 128:(j + 1) * 128] for j in range(4)], IDF[0:32, 0:32]),
             reads=[r_stg1c, r_id], writes=[r_psf[b]])
        K.op("act", lambda e, b=b: e.activation(
            out=QS[:, :, :, 0:2], in_=PSF[:, b, 0:128].rearrange("p (j s r) -> p j s r", j=4, r=2), func=AF.Copy),
            reads=[r_psf[b]], writes=[r_qs])
        rel_psf(b)

    def ffn_state_piece(pc):
        load_rows(STG1C, stf_d[:, pc * 512:(pc + 1) * 512], 32, r_stg1c, eng="pool")
        b = next_psf()
        K.op("pe", tr_group([PSF[:, b, q * 32:(q + 1) * 32] for q in range(4)],
                            [STG1C[0:32, q * 128:(q + 1) * 128] for q in range(4)], IDF[0:32, 0:32]),
             reads=[r_stg1c, r_id], writes=[r_psf[b]])
        K.op("act", lambda e, b=b, pc=pc: e.activation(
            out=STF[:, pc * 4:(pc + 1) * 4, :], in_=PSF[:, b, 0:128].rearrange("p (q r) -> p q r", q=4),
            func=AF.Copy), reads=[r_psf[b]], writes=[r_stf])
        rel_psf(b)

    def ring_load(pieces):
        i = next_ring()
        slot = RING[i]

        def fn(e):
            res = []
            for (c0, n, src) in pieces:
                res.append(e.dma_start(out=slot[:, :, c0:c0 + n], in_=src.rearrange("(kc p) n -> p kc n", p=128)))
            return res
        K.dma("pool", fn, s_ring[i], len(pieces), writes=[r_ring[i]])
        return i

    def adaln_vec(col0, kind, GT_out, bcol=None, gcol=None, slot=None):
        i = ring_load([(0, 512, wada_d[:, col0:col0 + 512])]) if slot is None else slot
        b = next_psf()
        K.op("pe", mm_group(PSF[0:18, b, :], [SC[:, kc, :] for kc in range(KC)], [RING[i][:, kc, :] for kc in range(KC)]),
             reads=[r_sc, r_ring[i]], writes=[r_psf[b]])
        K.op("act", lambda e, b=b: e.activation(out=MROW[0:18, :], in_=PSF[0:18, b, :], func=AF.Copy),
             reads=[r_psf[b]], writes=[r_mrow])
        rel_psf(b)
        b2 = next_psf()
        K.op("pe", tr_group([PSF[:, b2, q * 18:(q + 1) * 18] for q in range(4)],
                            [MROW[0:18, q * 128:(q + 1) * 128] for q in range(4)], IDF[0:18, 0:18]),
             reads=[r_mrow, r_id], writes=[r_psf[b2]])
        src = PSF[:, b2, 0:72].rearrange("p (q s) -> p q s", q=4)[:, :, 0:17]
        dst, c0 = GT_out
        if kind == "scale":
            fns = [lambda e: e.tensor_tensor(dst[:, c0:c0 + 4, :], src, VD[:, bcol:bcol + 4].unsqueeze(2).to_broadcast([128, 4, 17]), ALU.add),
                   lambda e: e.tensor_tensor(dst[:, c0:c0 + 4, :], dst[:, c0:c0 + 4, :],
                                             VECS[:, gcol:gcol + 4].unsqueeze(2).to_broadcast([128, 4, 17]), ALU.mult)]
        else:
            fns = [lambda e: e.tensor_tensor(dst[:, c0:c0 + 4, :], src,
                                             VECS[:, bcol:bcol + 4].unsqueeze(2).to_broadcast([128, 4, 17]), ALU.add)]
        chain(K, "dve", fns, [r_psf[b2], r_vd2, r_vecs], [r_mod])
        rel_psf(b2)

    def adaln_gate(col0, row_b, row_g, dstP=None, dstS=None, r_dst=None):
        sem = next_misc()
        gx = next_xt()
        ROWG = XT[gx]
        K.dma("sp", lambda e: [e.dma_start(out=ROWB, in_=rows_d[row_b:row_b + 1, :].to_broadcast([128, D])),
                               e.dma_start(out=ROWG, in_=rows_d[row_g:row_g + 1, :].to_broadcast([128, D]))],
              sem, 2, writes=[r_tmp, r_xt[gx]])
        for hb in range(2):
            i = ring_load([(0, 512, wada_d[:, col0 + hb * 512:col0 + (hb + 1) * 512])])
            for (M, m0, dst) in ((128, 0, dstP), (TS, 128, dstS)):
                b = next_psf()
                K.op("pe", mm_group(PSF[0:M, b, :], [REP[:, kc, m0:m0 + M] for kc in range(KC)],
                                    [RING[i][:, kc, :] for kc in range(KC)]),
                     reads=[r_rep, r_ring[i]], writes=[r_psf[b]])
                sl = slice(hb * 512, (hb + 1) * 512)
                chain(K, "dve", [lambda e, b=b, M=M, dst=dst, sl=sl: e.tensor_tensor(dst[0:M, sl], PSF[0:M, b, :], ROWB[0:M, sl], ALU.add),
                                 lambda e, M=M, dst=dst, sl=sl: e.tensor_tensor(dst[0:M, sl], dst[0:M, sl], ROWG[0:M, sl], ALU.mult)],
                      [r_psf[b], r_tmp, r_xt[gx]], [r_dst])
                rel_psf(b)
        release_xt(gx)

    def tile_rows(t):
        return (t * 128, 128) if t < 16 else (T, TS)

    ST_SS1, ST_RS1, ST_SSM, ST_RSM, ST_SS2, ST_RS2, ST_SSF, ST_RSF = range(8)
    r_stat = [[R("stat%d_%d" % (k, t)) for t in range(17)] for k in range(8)]

    def rms_stats(src_ap, rows, k_ss, k_rs, t, src_regs):
        K.op("act", lambda e: e.activation(out=JUNK[0:rows, :], in_=src_ap, func=AF.Square,
                                           accum_out=STAT[0:rows, k_ss, t:t + 1]),
             reads=src_regs, writes=[r_stat[k_ss][t]])
        K.op("act", lambda e: e.activation(out=STAT[0:rows, k_rs, t:t + 1], in_=STAT[0:rows, k_ss, t:t + 1],
                                           func=AF.Sqrt, bias=EPSB[0:rows, :], scale=1.0 / D),
             reads=[r_stat[k_ss][t], r_vd], writes=[r_stat[k_rs][t]])
        K.op("dve", lambda e: e.reciprocal(STAT[0:rows, k_rs, t:t + 1], STAT[0:rows, k_rs, t:t + 1]),
             reads=[r_stat[k_rs][t]], writes=[r_stat[k_rs][t]])

    def nf_scale(t, xi, k_rs, XNl, r_xnl):
        r0, rows = tile_rows(t)
        j = xn_i[0] % len(XNl)
        xn_i[0] += 1
        K.op("dve", lambda e: e.tensor_scalar(XNl[j][0:rows, :], XT[xi][0:rows, :], STAT[0:rows, k_rs, t:t + 1], None, ALU.mult),
             reads=[r_xt[xi], r_stat[k_rs][t]], writes=[r_xnl[j]])
        return j

    nf_bank = {}

    def nf_transpose(t, j, XNl, r_xnl, GT, STt, dstH, dcol0, r_dst, part="all", n_act=4):
        r0, rows = tile_rows(t)
        if part == "act":
            b = nf_bank.pop(t)
        else:
            b = next_psb()
            K.op("pe", tr_group([PSB[:, b, kc * 128:kc * 128 + rows] for kc in range(KC)],
                                [XNl[j][0:rows, kc * 128:(kc + 1) * 128] for kc in range(KC)], IDB[0:rows, 0:rows]),
                 reads=[r_xnl[j], r_idb], writes=[r_psb[b]])
            if part == "dve":
                nf_bank[t] = b
        src = PSB[:, b, :].rearrange("p (k n) -> p k n", k=KC)
        r_dst_a, r_dst_d = r_dst if isinstance(r_dst, tuple) else (r_dst, r_dst)
        if t < 16:
            def f_act(e):
                inst = None
                for kc in range(0, n_act):
                    inst = e.activation(out=dstH[:, kc, dcol0:dcol0 + 128], in_=src[:, kc, :], func=AF.Identity,
                                        bias=STt[:, kc, 0:1], scale=GT[:, kc, 0:1])
                return inst
            nd = KC - n_act
            if part in ("all", "act"):
                K.op("act", f_act, reads=[r_psb[b], r_mod], writes=[r_dst_a])
            if part in ("all", "dve"):
                chain(K, "dve", [lambda e: e.tensor_tensor(dstH[:, n_act:8, dcol0:dcol0 + 128], src[:, n_act:8, :],
                                                           GT[:, n_act:8, 0:1].to_broadcast([128, nd, 128]), ALU.mult),
                                 lambda e: e.tensor_tensor(dstH[:, n_act:8, dcol0:dcol0 + 128], dstH[:, n_act:8, dcol0:dcol0 + 128],
                                                           STt[:, n_act:8, 0:1].to_broadcast([128, nd, 128]), ALU.add)],
                      [r_psb[b], r_mod], [r_dst_d])
        else:
            o_ = dstH[:, :, dcol0:dcol0 + TS].rearrange("p k (s r) -> p k s r", r=DS)
            i_ = src[:, :, 0:TS].rearrange("p k (s r) -> p k s r", r=DS)
            chain(K, "dve", [lambda e: e.tensor_tensor(o_, i_, GT[:, :, 1:17].unsqueeze(3).to_broadcast([128, KC, NS, DS]), ALU.mult),
                             lambda e: e.tensor_tensor(o_, o_, STt[:, :, 1:17].unsqueeze(3).to_broadcast([128, KC, NS, DS]), ALU.add)],
                  [r_psb[b], r_mod], [r_dst])

    def load_x(t, src_d, reads=(), eng="sp"):
        r0, rows = tile_rows(t)
        xi = next_xt()
        K.dma(eng, lambda e: [e.dma_start(out=XT[xi][0:rows, :], in_=src_d[r0:r0 + rows, :])],
              s_xtp[xi] if eng == "pool" else s_xt[xi], 1, reads=reads, writes=[r_xt[xi]])
        return xi

    state_loads_early()
    adaln_vec(1 * D + 0, "scale", (G1T, 0), bcol=0, gcol=V_GPRE)
    adaln_vec(1 * D + 512, "scale", (G1T, 4), bcol=4, gcol=V_GPRE + 4)
    adaln_vec(0 * D + 0, "shift", (S1T, 0), bcol=V_BADA + 0)
    adaln_vec(0 * D + 512, "shift", (S1T, 4), bcol=V_BADA + 4)
    state_transposes_early()

    r_h = [R("h%d" % t) for t in range(17)]
    p1a_A = {}
    p1a_state = {"a": 0, "b": 0}

    def p1a_stage_a(t):
        r0, rows = tile_rows(t)
        xi = load_x(t, x_d)
        rms_stats(XT[xi][0:rows, :], rows, ST_SS1, ST_RS1, t, [r_xt[xi]])
        p1a_A[t] = nf_scale(t, xi, ST_RS1, XN1, r_xn1)
        release_xt(xi)

    def p1a_stage_b(t):
        r0, rows = tile_rows(t)
        nf_transpose(t, p1a_A[t], XN1, r_xn1, G1T, S1T, AH, r0, r_h[t])

    def p1a_advance(upto_b):
        while p1a_state["b"] < min(upto_b, 17):
            while p1a_state["a"] < 17 and p1a_state["a"] < p1a_state["b"] + NXN1 - 1:
                p1a_stage_a(p1a_state["a"])
                p1a_state["a"] += 1
            p1a_stage_b(p1a_state["b"])
            p1a_state["b"] += 1

    for _ in range(NXN1 - 1):
        p1a_stage_a(p1a_state["a"])
        p1a_state["a"] += 1

    COLT = [(0, 512), (512, 512), (1024, 512), (1536, 512), (T, TS)]
    r_by = [[R("by%d_%d" % (kc, c)) for c in range(5)] for kc in range(KC)]

    def h_regs(c):
        return [r_h[16]] if c == 4 else r_h[4 * c:4 * c + 4]

    r_tb = [R("tb%d" % i) for i in range(NTB)]
    r_db = [R("db%d" % i) for i in range(3)]
    for i_ in (8, 9):
        K.op("pool", lambda e, i_=i_: e.memset(TBUF[i_], 0.0), writes=[r_tb[i_]])

    wi_pool = ring_load([(0, 512, win_d[:, 0:512])])
    SBW = NS * (PB + DS)
    pool_items = [(g, c) for c in range(5) for g in range(4)]
    pool_ctx = {}

    def pool_stage_a(n):
        g, c = pool_items[n]
        w = WIN[g]
        nlev = g + 1
        c0, N = COLT[c]
        samp = (c == 4)
        need = 17 if samp else 4 * (c + 1)
        p1a_advance(min(17, need + g + 1))
        vi = 2 * g + c % 2
        V, rV = TBUF[vi], r_tb[vi]
        Vp, rVp = TBUF[2 * g + 1 - c % 2], r_tb[2 * g + 1 - c % 2]
        b = next_psf()
        K.op("pe", mm_group(PSF[:, b, 0:N], [RING[wi_pool][:, kc, g * 128:(g + 1) * 128] for kc in range(KC)],
                            [AH[:, kc, c0:c0 + N] for kc in range(KC)]),
             reads=[r_ring[wi_pool]] + h_regs(c), writes=[r_psf[b]])
        if not samp:
            Wd = 16 + N

            def f_ev(e, V=V, Vp=Vp, b=b, c=c, N=N):
                if c == 0:
                    e.memzero(V[:, 0:16])
                else:
                    e.activation(out=V[:, 1:16], in_=Vp[:, 513:528], func=AF.Copy)
                return e.activation(out=V[:, 16:16 + N], in_=PSF[:, b, 0:N], func=AF.Copy)
            K.op("act", f_ev, reads=[r_psf[b]] + ([rVp] if c else []), writes=[rV])
            rel_psf(b)
        else:
            Wd = SBW

            def f_ev(e, V=V, b=b, g=g):
                v3 = V[:, 0:SBW].rearrange("p (s r) -> p s r", r=PB + DS)
                e.activation(out=v3[:, :, 0:PB], in_=VS[:, g, :, 0:PB], func=AF.Copy)
                e.activation(out=v3[:, :, PB:PB + DS], in_=PSF[:, b, 0:TS].rearrange("p (s r) -> p s r", r=DS), func=AF.Copy)
                return e.activation(out=SMP[:, g, :], in_=PSF[:, b, 0:TS], func=AF.Copy)
            K.op("act", f_ev, reads=[r_psf[b], r_vs], writes=[rV, r_smp[g]])
            rel_psf(b)
        need_lo = 16 if not samp else 0
        bounds = []
        cur_lo = need_lo
        for lev in range(nlev, 0, -1):
            sh = 1 << (lev - 1)
            bounds.append((lev, sh, max(cur_lo, sh)))
            cur_lo = max(cur_lo - sh, 0)
        bounds.reverse()
        srcb, rsrc = V, rV
        pp = [(TBUF[8], r_tb[8]), (TBUF[9], r_tb[9])]
        for n_, (lev, sh, lo_) in enumerate(bounds):
            dstb, rdst = pp[n_ % 2]
            K.op("dve", lambda e, dstb=dstb, srcb=srcb, lo_=lo_, sh=sh, Wd=Wd: e.tensor_tensor(
                dstb[:, lo_:Wd], srcb[:, lo_:Wd], srcb[:, lo_ - sh:Wd - sh], ALU.add), reads=[rsrc], writes=[rdst])
            srcb, rsrc = dstb, rdst
        Sw, rSw = srcb, rsrc
        di = n % 3
        Dt, rD = DBUF[di], r_db[di]
        if not samp:
            fns = [lambda e, Sw=Sw, V=V, Dt=Dt, w=w, N=N: e.scalar_tensor_tensor(
                Dt[:, 0:N], Sw[:, 16:16 + N], 1.0 / w, V[:, 16:16 + N], ALU.mult, ALU.subtract)]
            if c == 0:
                fns.append(lambda e, Sw=Sw, g=g: e.tensor_tensor(Sw[:, 16:32], Sw[:, 16:32], INVC[:, g, :], ALU.mult))
                fns.append(lambda e, Sw=Sw, V=V, Dt=Dt: e.tensor_tensor(Dt[:, 0:16], Sw[:, 16:32], V[:, 16:32], ALU.subtract))
            chain(K, "dve", fns, [rSw, rV, r_invc], [rD, rSw])
            rhs = Dt[:, 0:N]
        else:
            def f_d(e, Sw=Sw, V=V, Dt=Dt, w=w):
                s3 = Sw[:, 0:SBW].rearrange("p (s r) -> p s r", r=PB + DS)
                v3 = V[:, 0:SBW].rearrange("p (s r) -> p s r", r=PB + DS)
                return e.scalar_tensor_tensor(Dt[:, 0:TS].rearrange("p (s r) -> p s r", r=DS), s3[:, :, PB:PB + DS],
                                              1.0 / w, v3[:, :, PB:PB + DS], ALU.mult, ALU.subtract)
            K.op("dve", f_d, reads=[rSw, rV], writes=[rD])
            rhs = Dt[:, 0:TS]
        pool_ctx[n] = (rhs, rD, V, rV)

    def pool_stage_b(n):
        g, c = pool_items[n]
        c0, N = COLT[c]
        samp = (c == 4)
        rhs, rD, V, rV = pool_ctx.pop(n)
        b2 = next_psf()
        K.op("pe", mm_group(PSF[:, b2, 0:N], [POOLW[:, g, :]], [rhs]), reads=[r_poolw, rD], writes=[r_psf[b2]])
        K.op("act", lambda e, b2=b2, g=g, c0=c0, N=N: e.activation(
            out=BY[:, g, c0:c0 + N], in_=PSF[:, b2, 0:N], func=AF.Copy, scale=VECS[:, V_PSC + g:V_PSC + g + 1]),
            reads=[r_psf[b2], r_vecs], writes=[r_by[g][c]])
        rel_psf(b2)
        if c == 3:
            b3 = next_psf()
            K.op("pe", tr_group([PSF[0:PB, b3, 0:128]], [V[:, 16 + 512 - PB:16 + 512]], IDF),
                 reads=[rV, r_id], writes=[r_psf[b3]])
            K.op("act", lambda e, b3=b3, g=g: e.activation(out=STG[0:PB, g * 128:(g + 1) * 128],
                                                          in_=PSF[0:PB, b3, 0:128], func=AF.Copy),
                 reads=[r_psf[b3]], writes=[r_stg])
            rel_psf(b3)
        if samp:
            b3 = next_psf()
            K.op("pe", tr_group([PSF[0:TS, b3, 0:128]], [SMP[:, g, :]], IDF),
                 reads=[r_smp[g], r_id], writes=[r_psf[b3]])
            K.op("act", lambda e, b3=b3, g=g: e.activation(out=STG2[0:TS, g * 128:(g + 1) * 128],
                                                          in_=PSF[0:TS, b3, 0:128], func=AF.Copy),
                 reads=[r_psf[b3]], writes=[r_stg2])
            rel_psf(b3)

    for n in range(len(pool_items) + 2):
        if n < len(pool_items):
            pool_stage_a(n)
        if n >= 2:
            pool_stage_b(n - 2)
    out_ops.append(K.dma("sp", lambda e: [e.dma_start(out=npp_d, in_=STG[0:PB, :])], s_o_stg, 1, reads=[r_stg]))
    out_ops.append(K.dma("sp", lambda e: [e.dma_start(out=nps_d[s, PB - DS:PB, :], in_=STG2[s * DS:(s + 1) * DS, :])
                                          for s in range(NS)], s_o_stg2, NS, reads=[r_stg2]))
    out_ops.append(K.dma("sp", lambda e: [e.dma_start(
        out=nps_d[:, 0:PB - DS, :], in_=stp_d.rearrange("(s r) f -> s r f", r=PB)[:, DS:PB, :])], s_o_cp, 1))

    if STAGE == 2:
        dump("BY", BY[:, 0:4, :], [r_by[kc][c] for kc in range(4) for c in range(5)])
        return finish()
    adaln_gate(2 * D, 0, 1, GT1P, GT1S, r_gt1)

    ffn_pieces = list(range(11))
    K.dma("pool", lambda e: [e.dma_start(out=WO0, in_=wout_d[:, 0:512].rearrange("(kc p) n -> p kc n", p=128))],
          s_wo0, 1, writes=[r_wo0])
    conv_deferred = []
    ada_cols = [4 * D + 0, 4 * D + 512, 3 * D + 0, 3 * D + 512]
    for j in range(4):
        ada_slot = None
        wi = ring_load([(0, 128, win_d[:, 512 + j * 128:512 + (j + 1) * 128]),
                        (128, 128, win_d[:, 1024 + j * 128:1024 + (j + 1) * 128]),
                        (256, 128, win_d[:, 1536 + j * 128:1536 + (j + 1) * 128])])
        w0 = VECS[:, V_CW + 0 * 4 + j:V_CW + 0 * 4 + j + 1]
        w1 = VECS[:, V_CW + 1 * 4 + j:V_CW + 1 * 4 + j + 1]
        w2 = VECS[:, V_CW + 2 * 4 + j:V_CW + 2 * 4 + j + 1]
        for c in range(5):
            c0, N = COLT[c]
            samp = (c == 4)
            if c == 3:
                ada_slot = ring_load([(0, 512, wada_d[:, ada_cols[j]:ada_cols[j] + 512])])
            if ffn_pieces:
                ffn_state_piece(ffn_pieces.pop(0))
            bx, bc, bb = next_psf(), next_psf(), next_psf()
            for (bk, off) in ((bx, 0), (bc, 256), (bb, 128)):
                K.op("pe", mm_group(PSF[:, bk, 0:N], [RING[wi][:, kc, off:off + 128] for kc in range(KC)],
                                    [AH[:, kc, c0:c0 + N] for kc in range(KC)]),
                     reads=[r_ring[wi]] + h_regs(c), writes=[r_psf[bk]])
            while conv_deferred:
                conv_deferred.pop(0)()
            qi = 4 + (c % 2)
            Q, rQ = TBUF[qi], r_tb[qi]
            Qp, rQp = TBUF[4 + (1 - c % 2)], r_tb[4 + (1 - c % 2)]
            XC, rXC = TBUF[6], r_tb[6]
            ACC, rACC = TBUF[7], r_tb[7]
            K.op("act", lambda e, XC=XC, bx=bx, N=N: e.activation(out=XC[:, 0:N], in_=PSF[:, bx, 0:N], func=AF.Copy),
                 reads=[r_psf[bx]], writes=[rXC])
            if not samp:
                def f_q(e, Q=Q, Qp=Qp, XC=XC, bc=bc, c=c, N=N):
                    if c == 0:
                        e.memset(Q[:, 0:2], 0.0)
                    else:
                        e.tensor_copy(Q[:, 0:2], Qp[:, 512:514])
                    return e.tensor_tensor(Q[:, 2:2 + N], PSF[:, bc, 0:N], XC[:, 0:N], ALU.mult)
                K.op("dve", f_q, reads=[r_psf[bc], rXC] + ([rQp] if c else []), writes=[rQ])
                qv = [Q[:, k:k + N] for k in range(3)]
                accv = ACC[:, 0:N]
                bv = PSF[:, bb, 0:N]
                outv = BY[:, 4 + j, c0:c0 + N]
            else:
                q3 = Q[:, 0:NS * 6].rearrange("p (s r) -> p s r", r=6)

                def f_q(e, q3=q3, XC=XC, bc=bc, j=j):
                    e.tensor_copy(q3[:, :, 0:2], QS[:, j, :, 0:2])
                    return e.tensor_tensor(q3[:, :, 2:6], PSF[:, bc, 0:TS].rearrange("p (s r) -> p s r", r=DS),
                                           XC[:, 0:TS].rearrange("p (s r) -> p s r", r=DS), ALU.mult)
                K.op("dve", f_q, reads=[r_psf[bc], rXC, r_qs], writes=[rQ])
                K.op("dve", lambda e, q3=q3: e.tensor_copy(SMP2.rearrange("p (s r) -> p s r", r=2), q3[:, :, 4:6]),
                     reads=[rQ], writes=[r_smp2])
                qv = [q3[:, :, k:k + DS] for k in range(3)]
                accv = ACC[:, 0:TS].rearrange("p (s r) -> p s r", r=DS)
                bv = PSF[:, bb, 0:TS].rearrange("p (s r) -> p s r", r=DS)
                outv = BY[:, 4 + j, c0:c0 + TS].rearrange("p (s r) -> p s r", r=DS)
            chain(K, "dve", [
                lambda e, qv=qv, accv=accv, w2=w2: e.tensor_scalar(accv, qv[2], w2, None, ALU.mult),
                lambda e, qv=qv, accv=accv, w1=w1: e.scalar_tensor_tensor(accv, qv[1], w1, accv, ALU.mult, ALU.add),
                lambda e, qv=qv, accv=accv, w0=w0: e.scalar_tensor_tensor(accv, qv[0], w0, accv, ALU.mult, ALU.add)],
                [rQ, r_vecs], [rACC])
            K.op("dve", lambda e, accv=accv, bv=bv, outv=outv: e.tensor_tensor(outv, accv, bv, ALU.mult),
                 reads=[rACC, r_psf[bb]], writes=[r_by[4 + j][c]])
            rel_psf(bx, bc, bb)
            if c == 3:
                def _d1(Q=Q, rQ=rQ, j=j):
                    b3 = next_psf()
                    K.op("pe", tr_group([PSF[0:2, b3, 0:128]], [Q[:, 512:514]], IDF), reads=[rQ, r_id], writes=[r_psf[b3]])
                    K.op("act", lambda e, b3=b3, j=j: e.activation(out=STG[0:2, j * 128:(j + 1) * 128],
                                                                  in_=PSF[0:2, b3, 0:128], func=AF.Copy),
                         reads=[r_psf[b3]], writes=[r_stg])
                    rel_psf(b3)
                conv_deferred.append(_d1)
            if samp:
                def _d2(j=j):
                    b3 = next_psf()
                    K.op("pe", tr_group([PSF[0:2 * NS, b3, 0:128]], [SMP2], IDF),
                         reads=[r_smp2, r_id], writes=[r_psf[b3]])
                    K.op("act", lambda e, b3=b3, j=j: e.activation(out=STG2[0:2 * NS, j * 128:(j + 1) * 128],
                                                                  in_=PSF[0:2 * NS, b3, 0:128], func=AF.Copy),
                         reads=[r_psf[b3]], writes=[r_stg2])
                    rel_psf(b3)
                conv_deferred.append(_d2)
        if j == 0:
            adaln_vec(4 * D + 0, "scale", (G2T, 0), bcol=8, gcol=V_GFFN, slot=ada_slot)
        elif j == 1:
            adaln_vec(4 * D + 512, "scale", (G2T, 4), bcol=12, gcol=V_GFFN + 4, slot=ada_slot)
        elif j == 2:
            adaln_vec(3 * D + 0, "shift", (S2T, 0), bcol=V_BADA + 16, slot=ada_slot)
            adaln_gate(5 * D, 2, 3, GTP, GTS, r_gt)
        else:
            adaln_vec(3 * D + 512, "shift", (S2T, 4), bcol=V_BADA + 20, slot=ada_slot)
    while conv_deferred:
        conv_deferred.pop(0)()
    out_ops.append(K.dma("sp", lambda e: [e.dma_start(out=ncp_d, in_=STG[0:2, :])], s_o_stg, 1, reads=[r_stg]))
    out_ops.append(K.dma("sp", lambda e: [e.dma_start(out=ncs_d, in_=STG2[0:2 * NS, :])], s_o_stg2, 1, reads=[r_stg2]))

    if STAGE == 3:
        dump("BY", BY, [r_by[kc][c] for kc in range(KC) for c in range(5)])
        return finish()

    wo1 = ring_load([(0, 512, wout_d[:, 512:1024])])
    WOB = [WO0, RING[wo1]]
    r_wob = [r_wo0, r_ring[wo1]]
    r_yd = [R("yd%d" % t) for t in range(17)]
    GROUPS = [list(range(0, 8)), list(range(8, 17))]
    r_h2 = [R("h2_%d" % t) for t in range(17)]
    for t in range(17):
        r_h2[t].alias(r_h)

    def gcol(t):
        return (t % 8) * 128 if t < 16 else 1024

    tok_ctx = {}
    ffn_wi = [{}, {}]

    def token_stage_a(t, mm_lhs, mm_rhs_half, mm_reads, src_d, final, mm_reads2=None):
        r0, rows = tile_rows(t)
        b = next_psf_pair()
        nk = len(mm_lhs)
        h1 = nk // 2
        for hb in range(2):
            rhs = mm_rhs_half(hb)
            if mm_reads2 is None:
                K.op("pe", mm_group(PSF[0:rows, b + hb, :], mm_lhs, rhs), reads=mm_reads, writes=[r_psf[b + hb]])
            else:
                K.op("pe", mm_part(PSF[0:rows, b + hb, :], mm_lhs[:h1], rhs[:h1], True, False), reads=mm_reads, writes=[r_psf[b + hb]])
                K.op("pe", mm_part(PSF[0:rows, b + hb, :], mm_lhs[h1:], rhs[h1:], False, True), reads=mm_reads2, writes=[r_psf[b + hb]])
        xi = load_x(t, src_d, reads=([r_yd[t]] if final else []), eng="pool")
        tok_ctx[t] = (b, xi)

    def token_stage_b(t, k_ss, k_rs, final):
        r0, rows = tile_rows(t)
        b, xi = tok_ctx[t]
        if final:
            GTt, r_gtt = (GTP if t < 16 else GTS), r_gt
        else:
            GTt, r_gtt = (GT1P if t < 16 else GT1S), r_gt1
        psv = PSF[0:rows, b:b + 2, :].rearrange("p a n -> p (a n)")
        rms_stats(psv, rows, k_ss, k_rs, t, [r_psf[b], r_psf[b + 1]])
        K.op("dve", lambda e: e.scalar_tensor_tensor(TMP[0:rows, :], psv, STAT[0:rows, k_rs, t:t + 1], GTt[0:rows, :],
                                                     ALU.mult, ALU.mult),
             reads=[r_psf[b], r_psf[b + 1], r_stat[k_rs][t], r_gtt], writes=[r_tmp])
        rel_psf(b, b + 1)
        K.op("dve", lambda e: e.tensor_tensor(XT[xi][0:rows, :], TMP[0:rows, :], XT[xi][0:rows, :], ALU.add),
             reads=[r_tmp, r_xt[xi]], writes=[r_xt[xi]])
        o = K.dma("sp", lambda e: [e.dma_start(out=y_d[r0:r0 + rows, :], in_=XT[xi][0:rows, :])],
                  s_o_xt[xi] if final else next_misc(), 1, reads=[r_xt[xi]], writes=[r_yd[t]])
        if final:
            out_ops.append(o)
        return xi

    h2_ctx = {}

    def h2_stage_a(t, xi, XNl, r_xnl):
        r0, rows = tile_rows(t)
        rms_stats(XT[xi][0:rows, :], rows, ST_SS2, ST_RS2, t, [r_xt[xi]])
        h2_ctx[t] = nf_scale(t, xi, ST_RS2, XNl, r_xnl)

    def h2_stage_b(t, XNl, r_xnl, part="all", n_act=4):
        j = h2_ctx[t] if part == "dve" else h2_ctx.pop(t)
        nf_transpose(t, j, XNl, r_xnl, G2T, S2T, H2, gcol(t), r_h2[t], part=part, n_act=n_act)

    def p1c_a(t):
        r0, rows = tile_rows(t)
        c = 4 if t == 16 else t // 4
        token_stage_a(t, [BY[:, kc, r0:r0 + rows] for kc in range(KC)],
                      lambda hb: [WOB[hb][:, kc, :] for kc in range(KC)],
                      [r_by[kc][c] for kc in range(KC)] + r_wob, x_d, False)

    ffn_wi[0][0] = ring_load([(0, 256, wup_d[:, 0:256]), (256, 256, wup_d[:, DFF:DFF + 256])])
    for step in range(17 + 5):
        t = step - 4
        if 0 <= t < 17 and t in GROUPS[0]:
            h2_stage_b(t, XN1, r_xn1, part="dve", n_act=5)
        if step < 17:
            p1c_a(step)
        t = step - 1
        if 0 <= t < 17:
            xi = token_stage_b(t, ST_SSM, ST_RSM, False)
            tok_ctx[t] = (tok_ctx[t][0], xi)
        t = step - 2
        if 0 <= t < 17:
            if t in GROUPS[0]:
                h2_stage_a(t, tok_ctx[t][1], XN1, r_xn1)
            release_xt(tok_ctx[t][1])
        t = step - 4
        if 0 <= t < 17 and t in GROUPS[0]:
            h2_stage_b(t, XN1, r_xn1, part="act", n_act=5)
    while ffn_pieces:
        ffn_state_piece(ffn_pieces.pop(0))

    if STAGE == 4:
        dump("H2", H2, r_h2[0:8])
        return finish()

    p1_regs = ([r_by[kc][c] for kc in range(KC) for c in range(5)] + r_tb + r_db +
               [r_stg, r_stg2, r_stg1c, r_poolw, r_vs, r_qs, r_mrow, r_smp2, r_gt1, r_wo0] + r_smp + r_xn1 + r_h)
    r_g = [[R("g%d_%d" % (i, k)) for k in range(3)] for i in range(NPAIR)]
    r_wdn = R("wdn")
    r_t2 = [R("t2_%d" % i) for i in range(2 * NT2)]
    r_hb = [[R("hb%d_%d" % (ab, p)) for p in range(2)] for ab in range(2)]
    r_hw = [[R("hw%d_%d" % (ab, p)) for p in range(2)] for ab in range(2)]
    r_usb = [R("usb%d" % i) for i in range(2)]
    r_tsb = [R("tsb%d" % i) for i in range(2)]
    r_ub = r_usb + r_tsb + [x for l in r_hb for x in l] + [x for l in r_hw for x in l]
    r_stg3 = R("stg3")
    for rr in [x for l in r_g for x in l] + [r_wdn, r_stg3] + r_ub + r_t2 + r_xn2:
        rr.alias(p1_regs)
    r_halo = [R("halo%d" % ch) for ch in range(NCH)]
    r_halob = [R("halob%d" % ch) for ch in range(NCH)]
    s_wdn = new_sem("wdn")

    r_wdnq = [R("wdn%d" % q) for q in range(4)]
    s_wdnq = [new_sem("wdnq%d" % q) for q in range(4)]
    for rr in r_wdnq:
        rr.alias(p1_regs)

    def load_wdn(q):
        K.dma("pool", lambda e: [e.dma_start(out=WDN[:, q * 6:min(NPAIR, (q + 1) * 6), :],
                                             in_=wdn_d[q * 768:min(DFF, (q + 1) * 768), :].rearrange("(kc p) n -> p kc n", p=128))],
              s_wdnq[q], 1, writes=[r_wdnq[q]])

    GCT = [[(0, 0, 512, False), (1, 512, 512, False)],
           [(0, 1024, 512, False), (1, 1536, 512, False), (2, T, TS, True)]]

    def h2_regs(gi, k):
        if gi == 0:
            return r_h2[4 * k:4 * k + 4]
        return [r_h2[16]] if k == 2 else r_h2[8 + 4 * k:8 + 4 * k + 4]

    def ffn_prefetch(gi, pb):
        wi_of = ffn_wi[gi]
        if gi == 0 and pb in (2, 4, 6, 8):
            load_wdn(pb // 2 - 1)
        for pq in (pb, pb + 1, pb + 2):
            if pq < NPAIR // 2 and pq not in wi_of:
                wi_of[pq] = ring_load([(0, 256, wup_d[:, pq * 256:(pq + 1) * 256]),
                                       (256, 256, wup_d[:, DFF + pq * 256:DFF + (pq + 1) * 256])])

    def ffn_group(gi):
        ptiles = [(k, c0, N) for (k, c0, N, samp) in GCT[gi] if not samp]
        has_samp = any(samp for (_, _, _, samp) in GCT[gi])
        items = []
        for pb in range(NPAIR // 2):
            for ii in range(2):
                for (k, c0, N) in ptiles:
                    items.append((pb, ii, k, c0, N))
        ctx = {}
        wi_of = ffn_wi[gi]

        def prefetch(pb):
            ffn_prefetch(gi, pb)

        def st_a(n):
            pb, ii, k, c0, N = items[n]
            i = pb * 2 + ii
            if (ii, k) == (0, 0):
                prefetch(pb)
            wi = wi_of[pb]
            gc0 = c0 - 1024 * gi
            bnk = [next_psf(), next_psf()]
            for ab in range(2):
                K.op("pe", mm_group(PSF[:, bnk[ab], 0:N],
                                    [RING[wi][:, kc, ab * 256 + ii * 128:ab * 256 + (ii + 1) * 128] for kc in range(KC)],
                                    [H2[:, kc, gc0:gc0 + N] for kc in range(KC)]),
                     reads=[r_ring[wi]] + h2_regs(gi, k), writes=[r_psf[bnk[ab]]])
            hpar = i % 2
            info = []
            pre_ops, main_ops, post_ops = [], [], []
            for ab in range(2):
                ch = ab * NPAIR + i
                Tt, rT = TB2[ab * NT2 + n % NT2], r_t2[ab * NT2 + n % NT2]
                fw = [VECS[:, V_FW + kk * NCH + ch:V_FW + kk * NCH + ch + 1] for kk in range(3)]
                bk = bnk[ab]
                first = (gi == 0 and k == 0)
                from_halo = (gi == 1 and k == 0)
                last = (k == len(ptiles) - 1)
                if first:
                    hsrc, r_hsrc = None, None
                elif from_halo:
                    hsrc, r_hsrc = HALO[:, ch, :], r_halo[ch]
                else:
                    hsrc, r_hsrc = HBUF[:, ab, hpar, :], r_hb[ab][hpar]
                if last and gi == 0:
                    hdst, r_hdst = HALO[:, ch, :], r_halo[ch]
                elif last:
                    hdst, r_hdst = HALOB[:, ch, :], r_halob[ch]
                else:
                    hdst, r_hdst = HBUF[:, ab, hpar, :], r_hb[ab][hpar]
                tv = Tt[:, 0:N]
                hw1, r_hw1 = HW1[:, ab, hpar, 0:1], r_hw[ab][hpar]
                if from_halo:
                    pre_ops.append((lambda e, hw1=hw1, ch=ch, fw=fw: e.activation(out=hw1, in_=HALO[:, ch, 1:2], func=AF.Copy, scale=fw[1]),
                                    [r_halo[ch], r_vecs], [r_hw1]))

                def f_t0(e, bk=bk, N=N, tv=tv, fw=fw, hdst=hdst, use_h=(hsrc is not None), hw1=hw1):
                    e.activation(out=hdst, in_=PSF[:, bk, N - 2:N], func=AF.Copy)
                    if use_h:
                        e.activation(out=tv[:, 0:1], in_=PSF[:, bk, 0:1], func=AF.Identity, scale=fw[2], bias=hw1)
                        return e.activation(out=tv[:, 1:N], in_=PSF[:, bk, 1:N], func=AF.Copy, scale=fw[2])
                    return e.activation(out=tv, in_=PSF[:, bk, 0:N], func=AF.Copy, scale=fw[2])
                main_ops.append((f_t0, [r_psf[bk], r_vecs] + ([r_hw1] if hsrc is not None else []), [rT, r_hdst]))
                if not last:
                    post_ops.append((lambda e, hw1=hw1, hdst=hdst, fw=fw: e.activation(out=hw1, in_=hdst[:, 1:2], func=AF.Copy, scale=fw[1]),
                                     [r_hdst, r_vecs], [r_hw1]))
                info.append((bk, tv, fw, rT, hsrc, r_hsrc, N))
            for (f_, rd_, wr_) in pre_ops + main_ops + post_ops:
                K.op("act", f_, reads=rd_, writes=wr_)
            ctx[n] = info

        def st_b(n):
            per = []
            for (bk, tv, fw, rT, hsrc, r_hsrc, N) in ctx[n]:
                fns = [lambda e, bk=bk, tv=tv, fw=fw, N=N: e.scalar_tensor_tensor(
                           tv[:, 1:N], PSF[:, bk, 0:N - 1], fw[1], tv[:, 1:N], ALU.mult, ALU.add),
                       lambda e, bk=bk, tv=tv, fw=fw, N=N: e.scalar_tensor_tensor(
                           tv[:, 2:N], PSF[:, bk, 0:N - 2], fw[0], tv[:, 2:N], ALU.mult, ALU.add)]
                rd = [r_psf[bk], r_vecs]
                if hsrc is not None:
                    fns.append(lambda e, tv=tv, fw=fw, hsrc=hsrc: e.scalar_tensor_tensor(
                        tv[:, 0:2], hsrc[:, 0:2], fw[0], tv[:, 0:2], ALU.mult, ALU.add))
                    rd.append(r_hsrc)
                per.append((fns, rd, rT, bk))
            for j in range(max(len(p[0]) for p in per)):
                for (fns, rd, rT, bk) in per:
                    if j < len(fns):
                        K.op("dve", fns[j], reads=rd, writes=[rT])
            for (fns, rd, rT, bk) in per:
                rel_psf(bk)
            ta, rTa = ctx[n][0][1], ctx[n][0][3]
            K.op("act", lambda e, ta=ta: e.activation(out=ta, in_=ta, func=AF.Silu), reads=[rTa], writes=[rTa])

        def st_c(n):
            pb, ii, k, c0, N = items[n]
            i = pb * 2 + ii
            gc0 = c0 - 1024 * gi
            (bka, ta, fwa, rTa, _, _, _), (bkb, tb, fwb, rTb, _, _, _) = ctx.pop(n)
            gv = GT_[:, i, gc0:gc0 + N]
            K.op("pool", lambda e, gv=gv, ta=ta, tb=tb: e.tensor_tensor(gv, ta, tb, ALU.mult),
                 reads=[rTa, rTb], writes=[r_g[i][k]])

        def samp_mm(pb):
            wi = wi_of[pb]
            us, rus = USB[pb % 2], r_usb[pb % 2]
            K.op("pool", lambda e, us=us, pb=pb: e.tensor_copy(
                us[:, 0:2, :, 0:2], STF[:, 2 * pb:2 * pb + 2, :].rearrange("p c (s r) -> p c s r", r=2)),
                reads=[r_stf], writes=[rus])
            K.op("pool", lambda e, us=us, pb=pb: e.tensor_copy(
                us[:, 2:4, :, 0:2], STF[:, NPAIR + 2 * pb:NPAIR + 2 * pb + 2, :].rearrange("p c (s r) -> p c s r", r=2)),
                reads=[r_stf], writes=[rus])
            psx = PSB[:, 0, :].bitcast(F32)
            for q in range(4):
                ab, ii = q // 2, q % 2
                K.op("pe", mm_group(psx[:, q * TS:(q + 1) * TS],
                                    [RING[wi][:, kc, ab * 256 + ii * 128:ab * 256 + (ii + 1) * 128] for kc in range(KC)],
                                    [H2[:, kc, 1024:1024 + TS] for kc in range(KC)]),
                     reads=[r_ring[wi], r_h2[16]], writes=[r_psb[0]])
            K.op("act", lambda e, us=us, psx=psx: e.activation(
                out=us[:, :, :, 2:6], in_=psx[:, 0:4 * TS].rearrange("p (q s r) -> p q s r", q=4, r=DS), func=AF.Copy),
                reads=[r_psb[0]], writes=[rus])

        def samp_post(pb):
            us, rus = USB[pb % 2], r_usb[pb % 2]
            ts, rts = TSB[pb % 2], r_tsb[pb % 2]
            t4 = ts.rearrange("p c (s r) -> p c s r", r=DS)
            fns = []
            for ab in range(2):
                cs = slice(2 * ab, 2 * ab + 2)
                c0_ = ab * NPAIR + 2 * pb

                def wv(kk, c0_=c0_):
                    return VECS[:, V_FW + kk * NCH + c0_:V_FW + kk * NCH + c0_ + 2].unsqueeze(2).unsqueeze(3).to_broadcast([128, 2, NS, DS])
                fns.append(lambda e, cs=cs, wv=wv: e.tensor_tensor(t4[:, cs], us[:, cs, :, 2:6], wv(2), ALU.mult))
            chain(K, "pool", fns, [rus, r_vecs], [rts])
            tmp, rtmp = TSB[1 - pb % 2], r_tsb[1 - pb % 2]
            tm4 = tmp.rearrange("p c (s r) -> p c s r", r=DS)
            for kk in (1, 0):
                for ab in range(2):
                    cs = slice(2 * ab, 2 * ab + 2)
                    c0_ = ab * NPAIR + 2 * pb
                    wvk = VECS[:, V_FW + kk * NCH + c0_:V_FW + kk * NCH + c0_ + 2].unsqueeze(2).unsqueeze(3).to_broadcast([128, 2, NS, DS])
                    K.op("pool", lambda e, cs=cs, wvk=wvk, kk=kk: e.tensor_tensor(tm4[:, cs], us[:, cs, :, kk:kk + DS], wvk, ALU.mult),
                         reads=[rus, r_vecs], writes=[rtmp])
                    K.op("pool", lambda e, cs=cs: e.tensor_tensor(t4[:, cs], t4[:, cs], tm4[:, cs], ALU.add),
                         reads=[rtmp, rts], writes=[rts])
            K.op("act", lambda e: e.activation(out=ts[:, 0:2, :], in_=ts[:, 0:2, :], func=AF.Silu), reads=[rts], writes=[rts])
            K.op("pool", lambda e, pb=pb: e.tensor_tensor(GT_[:, 2 * pb:2 * pb + 2, 1024:1024 + TS], ts[:, 0:2, :], ts[:, 2:4, :], ALU.mult),
                 reads=[rts], writes=[r_g[2 * pb][2], r_g[2 * pb + 1][2]])
            K.op("pool", lambda e, us=us, pb=pb: e.tensor_copy(
                STF[:, 2 * pb:2 * pb + 2, :].rearrange("p c (s r) -> p c s r", r=2), us[:, 0:2, :, 4:6]),
                reads=[rus], writes=[r_stf])
            K.op("pool", lambda e, us=us, pb=pb: e.tensor_copy(
                STF[:, NPAIR + 2 * pb:NPAIR + 2 * pb + 2, :].rearrange("p c (s r) -> p c s r", r=2), us[:, 2:4, :, 4:6]),
                reads=[rus], writes=[r_stf])

        NI = len(items)
        per_pb = 2 * len(ptiles)
        for step in range(NI + 2):
            if 0 <= step - 2 < NI:
                st_c(step - 2)
            if step < NI:
                st_a(step)
                if has_samp and step % per_pb == per_pb - 1:
                    samp_mm(step // per_pb)
            if 0 <= step - 1 < NI:
                st_b(step - 1)
                if has_samp and (step - 1) % per_pb == per_pb - 1:
                    samp_post((step - 1) // per_pb)

    def down_a(t):
        r0, rows = tile_rows(t)
        k = 2 if t == 16 else (t % 8) // 4
        g0 = gcol(t)
        token_stage_a(t, [GT_[:, i, g0:g0 + rows] for i in range(NPAIR)],
                      lambda hb: [WDN[:, i, hb * 512:(hb + 1) * 512] for i in range(NPAIR)],
                      [r_g[i][k] for i in range(NPAIR // 2)] + r_wdnq[0:2], y_d, True,
                      mm_reads2=[r_g[i][k] for i in range(NPAIR // 2, NPAIR)] + r_wdnq[2:4])

    ffn_group(0)
    if STAGE == 5:
        dump("G", GT_, [r_g[i][k] for i in range(NPAIR) for k in range(2)])
        return finish()

    lazy = list(GROUPS[1])
    lazy_ctx = {}

    def lazy_a(tt):
        if tt < 16:
            r_h2[tt].alias([r_h2[tt - 8]])
        xi = load_x(tt, y_d, reads=[r_yd[tt]], eng="pool")
        h2_stage_a(tt, xi, XN2, r_xn2)
        release_xt(xi)

    def lazy_b(tt):
        h2_stage_b(tt, XN2, r_xn2)

    g0t = GROUPS[0]
    la = 0
    lb = 0
    for step in range(len(g0t) + 1):
        if step == 3:
            ffn_wi[1][0] = ring_load([(0, 256, wup_d[:, 0:256]), (256, 256, wup_d[:, DFF:DFF + 256])])
            ffn_wi[1][1] = ring_load([(0, 256, wup_d[:, 256:512]), (256, 256, wup_d[:, DFF + 256:DFF + 512])])
        if step < len(g0t):
            down_a(g0t[step])
        for _ in range(2):
            if la < len(lazy) and la < lb + 2:
                lazy_a(lazy[la])
                la += 1
        if step >= 1:
            release_xt(token_stage_b(g0t[step - 1], ST_SSF, ST_RSF, True))
        for _ in range(2):
            if step >= 1 and lb < la and lb < len(lazy) and (lb < la - 1 or la == len(lazy)):
                lazy_b(lazy[lb])
                lb += 1
    while lb < len(lazy):
        if la < len(lazy):
            lazy_a(lazy[la])
            la += 1
        lazy_b(lazy[lb])
        lb += 1

    ffn_group(1)

    def ffn_state_out(pc):
        b = next_psf()
        K.op("pe", tr_group([PSF[0:2, b, q * 128:(q + 1) * 128] for q in range(4)],
                            [HALOB[:, _ch(pc, q), :] for q in range(4)], IDF),
             reads=[r_halob[_ch(pc, q)] for q in range(4)] + [r_id], writes=[r_psf[b]])
        b2 = next_psf()
        K.op("pe", tr_group([PSF[0:2 * NS, b2, q * 128:(q + 1) * 128] for q in range(4)],
                            [STF[:, _ch(pc, q), :] for q in range(4)], IDF), reads=[r_stf, r_id], writes=[r_psf[b2]])
        K.op("act", lambda e, b=b: e.activation(out=STG3[0:2, :], in_=PSF[0:2, b, :], func=AF.Copy),
             reads=[r_psf[b]], writes=[r_stg3])
        rel_psf(b)
        out_ops.append(K.dma("sp", lambda e, pc=pc: [e.dma_start(out=nfp_d[:, pc * 512:(pc + 1) * 512], in_=STG3[0:2, :])],
                             s_o_stg3, 1, reads=[r_stg3]))
        r_t = r_t2[pc % 2]
        tb = TB2[pc % 2]
        K.op("act", lambda e, b2=b2, tb=tb: e.activation(out=tb[0:2 * NS, 0:512], in_=PSF[0:2 * NS, b2, :], func=AF.Copy),
             reads=[r_psf[b2]], writes=[r_t])
        rel_psf(b2)
        out_ops.append(K.dma("sp", lambda e, pc=pc, tb=tb: [e.dma_start(out=nfs_d[:, pc * 512:(pc + 1) * 512],
                                                                        in_=tb[0:2 * NS, 0:512])],
                             s_o_tb[pc % 2], 1, reads=[r_t]))

    g1t = GROUPS[1]
    so_pieces = list(range(11))
    for step in range(len(g1t) + 1):
        if step < len(g1t):
            down_a(g1t[step])
        if step >= 1:
            release_xt(token_stage_b(g1t[step - 1], ST_SSF, ST_RSF, True))
        if step == 1:
            while so_pieces:
                ffn_state_out(so_pieces.pop(0))

    return finish()


def _ch(pc, q):
    return pc * 4 + q


_NC_CACHE = {}


def _fm(v, n):
    return np.ascontiguousarray(np.asarray(v, np.float32).reshape(n, 128).T)


def kernel(x_prompt, x_sample, state_pool, state_conv, state_ffn, c_prompt, c_sample,
           w_ada, b_ada, g_pre_mix, g_post_mix, g_pre_ffn, g_post_ffn, w_in, pool_w,
           pool_scale, conv_w, w_out, ffn_w_up, ffn_conv_w, ffn_w_down):
    f = lambda a: np.ascontiguousarray(np.asarray(a, dtype=np.float32))
    x_prompt, x_sample = f(x_prompt), f(x_sample)
    state_pool, state_conv, state_ffn = f(state_pool)[0], f(state_conv)[0], f(state_ffn)[0]
    c_prompt, c_sample = f(c_prompt), f(c_sample)
    b_ada_ = f(b_ada)[0]
    conv_w_ = f(conv_w)[0]
    fcw = f(ffn_conv_w)[0]
    vecs = np.concatenate([
        _fm(f(g_pre_mix)[0], 8), _fm(f(g_pre_ffn)[0], 8),
        _fm(b_ada_[0:D], 8), _fm(b_ada_[D:2 * D], 8), _fm(b_ada_[3 * D:4 * D], 8), _fm(b_ada_[4 * D:5 * D], 8),
        _fm(f(pool_scale)[0], 4),
        _fm(conv_w_[0], 4), _fm(conv_w_[1], 4), _fm(conv_w_[2], 4),
        _fm(fcw[0], NCH), _fm(fcw[1], NCH), _fm(fcw[2], NCH)], axis=1)
    assert vecs.shape == (128, NV)
    rows = np.stack([b_ada_[2 * D:3 * D], f(g_post_mix)[0], b_ada_[5 * D:6 * D], f(g_post_ffn)[0]], axis=0)
    shared = {
        "w_ada": f(w_ada)[0], "w_in": f(w_in)[0], "pool_w": f(pool_w)[0].reshape(512, 128),
        "w_out": f(w_out)[0], "w_up": f(ffn_w_up)[0], "w_down": f(ffn_w_down)[0],
        "vecs": np.ascontiguousarray(vecs), "rows": np.ascontiguousarray(rows),
        "ident": np.eye(128, dtype=np.float32),
    }
    in_maps = []
    for i in range(NCORES):
        sl = slice(i * NS, (i + 1) * NS)
        m = dict(shared)
        m["x"] = np.ascontiguousarray(np.concatenate([x_prompt[i], x_sample[sl].reshape(TS, D)], axis=0))
        m["cT"] = np.ascontiguousarray(np.concatenate([c_prompt[i:i + 1], c_sample[sl], np.zeros((1, D), np.float32)], axis=0).T)
        m["st_pool"] = np.ascontiguousarray(state_pool[sl].reshape(NS * PB, 512))
        m["st_conv"] = np.ascontiguousarray(state_conv[sl].reshape(NS * 2, 512))
        m["st_ffn"] = np.ascontiguousarray(state_ffn[sl].reshape(NS * 2, 2 * DFF))
        in_maps.append(m)
    if "nc" not in _NC_CACHE:
        _NC_CACHE["nc"] = build_program()
    nc = _NC_CACHE["nc"]
    res = run_bass_kernel_spmd(nc, in_maps, core_ids=list(range(NCORES)))
    rs = res.results
    y_p = np.stack([rs[i]["y"][0:T] for i in range(NCORES)], axis=0)
    y_s = np.concatenate([rs[i]["y"][T:NTOK].reshape(NS, DS, D) for i in range(NCORES)], axis=0)
    npp = np.stack([rs[i]["npp"] for i in range(NCORES)], axis=0)[None]
    ncp = np.stack([rs[i]["ncp"] for i in range(NCORES)], axis=0)[None]
    nfp = np.stack([rs[i]["nfp"] for i in range(NCORES)], axis=0)[None]
    nps = np.concatenate([rs[i]["nps"] for i in range(NCORES)], axis=0)[None]
    ncs = np.concatenate([rs[i]["ncs"].reshape(NS, 2, 512) for i in range(NCORES)], axis=0)[None]
    nfs = np.concatenate([rs[i]["nfs"].reshape(NS, 2, 2 * DFF) for i in range(NCORES)], axis=0)[None]
    return tuple(np.ascontiguousarray(a.astype(np.float32)) for a in (y_p, y_s, npp, ncp, nfp, nps, ncs, nfs))
```

```python
import numpy as np
from contextlib import ExitStack

import concourse.bass as bass
import concourse.mybir as mybir
from concourse.bass_utils import run_bass_kernel_spmd

F32 = mybir.dt.float32
BF16 = mybir.dt.bfloat16
AF = mybir.ActivationFunctionType
ALU = mybir.AluOpType

NCORES = 8
D = 1024
KC = 8
T = 2048
NS = 16
DS = 4
TS = NS * DS
NTOK = T + TS
DFF = 2816
NPAIR = 22
NCH = 44
EPS = 1e-6
WIN = (2, 4, 8, 16)
PB = 15
NV = 196

V_GPRE = 0
V_GFFN = 8
V_BADA = 16
V_PSC = 48
V_CW = 52
V_FW = 64


class Region:
    __slots__ = ("name", "w", "r")

    def __init__(self, name=""):
        self.name = name
        self.w = None
        self.r = []

    def alias(self, olds):
        for o in olds:
            if o.w is not None:
                self.r.append(o.w)
            self.r.extend(o.r)


class Sem:
    __slots__ = ("h", "count")

    def __init__(self, h):
        self.h = h
        self.count = 0


class Op:
    __slots__ = ("eng", "fn", "deps", "sig", "need", "dma", "sem", "n")


class Sched:
    ENGS = ("pe", "act", "dve", "pool", "sp")

    def __init__(self):
        self.ops = {e: [] for e in self.ENGS}
        self.all = []

    def _mk(self, eng, fn, reads, writes, deps):
        o = Op()
        o.eng = eng
        o.fn = fn
        o.sig = None
        o.need = False
        o.dma = False
        o.sem = None
        o.n = 0
        d = set(deps)
        for r in reads:
            if r.w is not None:
                d.add(r.w)
        for w in writes:
            if w.w is not None:
                d.add(w.w)
            d.update(w.r)
        for r in reads:
            r.r.append(o)
        for w in writes:
            w.w = o
            w.r = []
        d.discard(o)
        o.deps = d
        self.ops[eng].append(o)
        self.all.append(o)
        return o

    def op(self, eng, fn, reads=(), writes=(), deps=()):
        return self._mk(eng, fn, reads, writes, deps)

    def dma(self, eng, fn, sem, n, reads=(), writes=(), deps=()):
        o = self._mk(eng, fn, reads, writes, deps)
        o.dma = True
        o.sem = sem
        o.n = n
        sem.count += 16 * n
        o.sig = (sem, sem.count)
        return o

    def finalize(self, esems):
        for o in self.all:
            for d in o.deps:
                d.need = True
        cnt = {e: 0 for e in self.ENGS}
        for o in self.all:
            if o.need and not o.dma:
                cnt[o.eng] += 1
                o.sig = (esems[o.eng], cnt[o.eng])

    def emit(self, eng, e):
        waited = {}
        for o in self.ops[eng]:
            need = {}
            for d in o.deps:
                s, v = d.sig
                if need.get(s, 0) < v:
                    need[s] = v
            for s, v in need.items():
                if waited.get(s, 0) < v:
                    e.wait_ge(s.h, v)
                    waited[s] = v
            if o.fn is None:
                continue
            r = o.fn(e)
            if o.dma:
                assert len(r) == o.n, (len(r), o.n)
                for i in r:
                    i.then_inc(o.sem.h, 16)
            elif o.need:
                r.then_inc(o.sig[0].h, 1)


def chain(K, eng, fns, reads, writes):
    o = None
    for f in fns:
        o = K.op(eng, f, reads=reads, writes=writes)
    return o


def mm_group(out_ap, lhs_list, rhs_list):
    def fn(e):
        n = len(lhs_list)
        inst = None
        for i in range(n):
            inst = e.matmul(out_ap, lhs_list[i], rhs_list[i], start=(i == 0), stop=(i == n - 1))
        return inst
    return fn


def mm_part(out_ap, lhs_list, rhs_list, first, last):
    def fn(e):
        n = len(lhs_list)
        inst = None
        for i in range(n):
            inst = e.matmul(out_ap, lhs_list[i], rhs_list[i], start=(first and i == 0), stop=(last and i == n - 1))
        return inst
    return fn


def tr_group(outs, ins, ident):
    def fn(e):
        inst = None
        for o, i in zip(outs, ins):
            inst = e.transpose(o, i, ident)
        return inst
    return fn


STAGE = 99


def build_program():
    nc = bass.Bass("TRN2", target_bir_lowering=False)
    es = ExitStack()

    def din(name, shape):
        return nc.dram_tensor(name, list(shape), F32, kind="ExternalInput").ap()

    def dout(name, shape):
        return nc.dram_tensor(name, list(shape), F32, kind="ExternalOutput").ap()

    x_d = din("x", [NTOK, D])
    cT_d = din("cT", [D, 18])
    stp_d = din("st_pool", [NS * PB, 512])
    stc_d = din("st_conv", [NS * 2, 512])
    stf_d = din("st_ffn", [NS * 2, 2 * DFF])
    wada_d = din("w_ada", [D, 6 * D])
    win_d = din("w_in", [D, 2048])
    pw_d = din("pool_w", [512, 128])
    wout_d = din("w_out", [D, D])
    wup_d = din("w_up", [D, 2 * DFF])
    wdn_d = din("w_down", [DFF, D])
    vecs_d = din("vecs", [128, NV])
    rows_d = din("rows", [4, D])
    ident_d = din("ident", [128, 128])

    y_d = dout("y", [NTOK, D])
    npp_d = dout("npp", [PB, 512])
    ncp_d = dout("ncp", [2, 512])
    nfp_d = dout("nfp", [2, 2 * DFF])
    nps_d = dout("nps", [NS, PB, 512])
    ncs_d = dout("ncs", [NS * 2, 512])
    nfs_d = dout("nfs", [NS * 2, 2 * DFF])

    ARENA_BYTES = 206 * 1024
    arena = es.enter_context(nc.sbuf_tensor("arena", [128, ARENA_BYTES // 2], BF16))
    cur = [0]

    def alloc(shape, dtype, at=None):
        esz = 4 if dtype == F32 else 2
        n = 1
        for s in shape[1:]:
            n *= s
        nbytes = n * esz
        nbytes = (nbytes + 31) // 32 * 32
        off = cur[0] if at is None else at
        if at is None:
            cur[0] += nbytes
        assert off + nbytes <= ARENA_BYTES, ("SBUF arena overflow", off, nbytes)
        ap = arena[:, off // 2: off // 2 + n * esz // 2]
        if dtype == F32:
            ap = ap.bitcast(F32)
        if len(shape) == 3:
            ap = ap.rearrange("p (a b) -> p a b", b=shape[2])
        elif len(shape) == 4:
            ap = ap.rearrange("p (a b c) -> p a b c", b=shape[2], c=shape[3])
        if shape[0] < 128:
            ap = ap[0:shape[0]]
        return ap

    VECS = alloc([128, NV], F32)
    VD = alloc([128, 64], F32)
    CTS = alloc([128, KC, 18], F32)
    SC = alloc([128, KC, 18], BF16)
    REP = alloc([128, KC, 192], BF16)
    G1T = alloc([128, KC, 17], F32)
    S1T = alloc([128, KC, 17], F32)
    G2T = alloc([128, KC, 17], F32)
    S2T = alloc([128, KC, 17], F32)
    IDF = alloc([128, 128], F32)
    IDB = alloc([128, 128], BF16)
    STAT = alloc([128, 8, 20], F32)
    INVC = alloc([128, 4, 16], F32)
    HALO = alloc([128, NCH, 2], F32)
    HALOB = alloc([128, NCH, 2], F32)
    STF = alloc([128, NCH, 32], F32)
    GTP = alloc([128, D], F32)
    GTS = alloc([128, D], F32)
    EPSB = alloc([128, 1], F32)
    RING_N = 3
    RING = [alloc([128, KC, 512], BF16) for _ in range(RING_N)]
    NXT = 4
    XT = [alloc([128, D], F32) for _ in range(NXT)]
    TMP = alloc([128, D], F32)
    JUNK = alloc([128, D], BF16)
    phase_mark = cur[0]
    AH = alloc([128, KC, NTOK], BF16)
    BY = alloc([128, KC, NTOK], BF16)
    POOLW = alloc([128, 4, 128], BF16)
    TBW = 528
    NTB = 10
    TBUF = [alloc([128, TBW], F32) for _ in range(NTB)]
    DBUF = [alloc([128, 512], BF16) for _ in range(3)]
    STG = alloc([128, 512], F32)
    STG2 = alloc([128, 512], F32)
    STG1C = alloc([128, 512], F32)
    VS = alloc([128, 4, NS, PB + DS], F32)
    QS = alloc([128, 4, NS, 2 + DS], F32)
    MROW = alloc([128, 512], F32)
    SMP = alloc([128, 4, TS], F32)
    SMP2 = alloc([128, 2 * NS], F32)
    NXN1 = 5
    WO0 = alloc([128, KC, 512], BF16)
    GT1P = alloc([128, D], F32)
    GT1S = alloc([128, D], F32)
    XN1 = [alloc([128, D], BF16) for _ in range(NXN1)]
    p1_end = cur[0]
    cur[0] = phase_mark
    GCOLS = 1024 + TS
    H2 = alloc([128, KC, GCOLS], BF16)
    GT_ = alloc([128, NPAIR, GCOLS], BF16)
    WDN = alloc([128, NPAIR, D], BF16)
    NT2 = 3
    TB2 = [alloc([128, 512], F32) for _ in range(2 * NT2)]
    HBUF = alloc([128, 2, 2, 2], F32)
    HW1 = alloc([128, 2, 2, 2], F32)
    USB = [alloc([128, 4, NS, 6], F32) for _ in range(2)]
    TSB = [alloc([128, 4, TS], F32) for _ in range(2)]
    STG3 = alloc([128, 512], F32)
    XN2 = [alloc([128, D], BF16) for _ in range(2)]
    p2_end = cur[0]
    ROWB = TMP
    print("SBUF bytes/partition: shared %d, phase1 %d, phase2 %d" % (phase_mark, p1_end, p2_end))

    PSF = es.enter_context(nc.psum_tensor("psf", [128, 6, 512], F32))
    PSB = es.enter_context(nc.psum_tensor("psb", [128, 2, D], BF16))

    def new_sem(name):
        return Sem(es.enter_context(nc.semaphore(name)))

    esems = {e: new_sem("e_" + e) for e in Sched.ENGS}
    K = Sched()

    R = Region
    r_ring = [R("ring%d" % i) for i in range(RING_N)]
    s_ring = [new_sem("ring%d" % i) for i in range(RING_N)]
    r_psf = [R("psf%d" % i) for i in range(6)]
    r_psb = [R("psb%d" % i) for i in range(2)]
    r_xt = [R("xt%d" % i) for i in range(NXT)]
    s_xt = [new_sem("xt%d" % i) for i in range(NXT)]
    s_xtp = [new_sem("xtp%d" % i) for i in range(NXT)]
    r_xn1 = [R("xn1_%d" % i) for i in range(NXN1)]
    r_xn2 = [R("xn2_%d" % i) for i in range(2)]
    xn_i = [0]
    r_tmp = R("tmp")
    r_junk = R("junk")
    r_vecs = R("vecs")
    r_vd = R("vd")
    r_sc = R("sc")
    r_rep = R("rep")
    r_mod = R("mod")
    r_gt = R("gt")
    r_gt1 = R("gt1")
    r_wo0 = R("wo0")
    s_wo0 = new_sem("wo0")
    r_id = R("id")
    r_idb = R("idb")
    r_mrow = R("mrow")
    r_vs = R("vs")
    r_qs = R("qs")
    r_stf = R("stf")
    r_stg1c = R("stg1c")
    r_invc = R("invc")
    r_smp = [R("smp%d" % i) for i in range(4)]
    r_smp2 = R("smp2")
    s_setup = new_sem("setup")
    s_setup_p = new_sem("setup_p")
    s_out = new_sem("out")
    s_o_stg = new_sem("o_stg")
    s_o_stg2 = new_sem("o_stg2")
    s_o_stg3 = new_sem("o_stg3")
    s_o_tb = [new_sem("o_tb0"), new_sem("o_tb1")]
    s_o_cp = new_sem("o_cp")
    s_o_xt = [new_sem("o_xt%d" % i) for i in range(NXT)]
    out_ops = []

    ring_i = [0]
    psf_i = [0]
    psb_i = [0]
    xt_i = [0]
    misc_i = [0]

    def next_ring():
        i = ring_i[0] % RING_N
        ring_i[0] += 1
        return i

    psf_free = list(range(6))

    def next_psf():
        assert psf_free, "out of PSUM banks"
        return psf_free.pop(0)

    def next_psf_pair():
        for b0 in list(psf_free):
            if b0 % 2 == 0 and b0 + 1 in psf_free:
                psf_free.remove(b0)
                psf_free.remove(b0 + 1)
                return b0
        raise AssertionError("out of PSUM bank pairs")

    def rel_psf(*bs):
        for b_ in bs:
            assert b_ not in psf_free
            psf_free.append(b_)

    def next_psb():
        i = psb_i[0] % 2
        psb_i[0] += 1
        return i

    xt_free = list(range(NXT))

    def next_xt():
        assert xt_free, "out of XT slots"
        return xt_free.pop(0)

    def release_xt(i):
        assert i not in xt_free
        xt_free.append(i)

    def next_misc():
        misc_i[0] += 1
        return new_sem("misc%d" % misc_i[0])

    dbg = {}

    def dump(name, ap, reads):
        shp = list(ap.shape)
        dt = nc.dram_tensor("dbg_" + name, shp, ap.dtype, kind="ExternalOutput").ap()
        dbg[name] = shp
        out_ops.append(K.dma("sp", lambda e: [e.dma_start(out=dt, in_=ap)], s_out, 1, reads=reads))

    def finish():
        K.op("sp", None, deps=out_ops)
        fix_setup()
        K.finalize(esems)
        with nc.Block() as block:
            @block.tensor
            def _(e):
                K.emit("pe", e)

            @block.scalar
            def _(e):
                K.emit("act", e)

            @block.vector
            def _(e):
                K.emit("dve", e)

            @block.gpsimd
            def _(e):
                K.emit("pool", e)

            @block.sync
            def _(e):
                K.emit("sp", e)
        es.close()
        nc._dbg = dbg
        return nc

    def setup_dma(eng, out_ap, in_ap, writes):
        return K.dma(eng, lambda e, o=out_ap, i=in_ap: [e.dma_start(out=o, in_=i)],
                     s_setup_p if eng == "pool" else s_setup, 1, writes=writes)

    setup_ops = []
    setup_ops.append(setup_dma("sp", VECS, vecs_d, [r_vecs]))
    setup_ops.append(setup_dma("sp", CTS, cT_d.rearrange("(kc p) s -> p kc s", p=128), [r_sc]))
    setup_ops.append(setup_dma("sp", IDF, ident_d, [r_id]))
    setup_ops.append(setup_dma("pool", IDB, ident_d, [r_idb]))
    r_poolw = R("poolw")
    setup_ops.append(setup_dma("pool", POOLW, pw_d.rearrange("(g c) d -> c g d", c=128), [r_poolw]))

    def fix_setup():
        for o in setup_ops:
            o.sig = (o.sem, o.sem.count)

    K.op("dve", lambda e: e.memset(EPSB, EPS), writes=[r_vd])

    def _invc(e):
        inst = None
        for g, w in enumerate(WIN):
            for t in range(w - 1):
                inst = e.memset(INVC[:, g, t:t + 1], 1.0 / (t + 1))
            inst = e.memset(INVC[:, g, w - 1:16], 1.0 / w)
        return inst
    K.op("dve", _invc, writes=[r_invc])

    def _vd(e):
        e.tensor_scalar(VD[:, 0:8], VECS[:, V_BADA + 8:V_BADA + 16], 1.0, None, ALU.add)
        return e.tensor_scalar(VD[:, 8:16], VECS[:, V_BADA + 24:V_BADA + 32], 1.0, None, ALU.add)
    r_vd2 = R("vd2")
    K.op("dve", _vd, reads=[r_vecs], writes=[r_vd2])

    K.op("act", lambda e: e.activation(out=SC, in_=CTS, func=AF.Silu), reads=[r_sc], writes=[r_sc])

    def _rep(e):
        e.tensor_copy(REP[:, :, 0:128], SC[:, :, 0:1].to_broadcast([128, KC, 128]))
        return e.tensor_copy(REP[:, :, 128:192].rearrange("p k (s r) -> p k s r", r=DS),
                             SC[:, :, 1:17].unsqueeze(3).to_broadcast([128, KC, NS, DS]))
    K.op("dve", _rep, reads=[r_sc], writes=[r_rep])

    def load_rows(dst, src, rows, rg, eng="sp"):
        sem = next_misc()
        return K.dma(eng, lambda e: [e.dma_start(out=dst[0:rows, :], in_=src)], sem, 1, writes=[rg])

    r_stg = R("stg")
    r_stg2 = R("stg2")

    def state_loads_early():
        load_rows(STG, stp_d[0:120, :], 120, r_stg)
        load_rows(STG2, stp_d[120:240, :], 120, r_stg2)
        load_rows(STG1C, stc_d, 32, r_stg1c)

    def state_transposes_early():
        for h, (stg, rg) in enumerate(((STG, r_stg), (STG2, r_stg2))):
            b = next_psf()
            K.op("pe", tr_group([PSF[:, b, g * 120:(g + 1) * 120] for g in range(4)],
                                [stg[0:120, g * 128:(g + 1) * 128] for g in range(4)], IDF[0:120, 0:120]),
                 reads=[rg, r_id], writes=[r_psf[b]])
            K.op("act", lambda e, b=b, h=h: e.activation(
                out=VS[:, :, h * 8:(h + 1) * 8, 0:PB],
                in_=PSF[:, b, 0:480].rearrange("p (g s r) -> p g s r", g=4, r=PB), func=AF.Copy),
                reads=[r_psf[b]], writes=[r_vs])
            rel_psf(b)
        b = next_psf()
        K.op("pe", tr_group([PSF[:, b, j * 32:(j + 1) * 32] for j in range(4)],
                            [STG1C[0:32, j * 128:(j + 1) * 128] for j in range(4)], IDF[0:32, 0:32]),
             reads=[r_stg1c, r_id], writes=[r_psf[b]])
        K.op("act", lambda e, b=b: e.activation(
            out=QS[:, :, :, 0:2], in_=PSF[:, b, 0:128].rearrange("p (j s r) -> p j s r", j=4, r=2), func=AF.Copy),
            reads=[r_psf[b]], writes=[r_qs])
        rel_psf(b)

    def ffn_state_piece(pc):
        load_rows(STG1C, stf_d[:, pc * 512:(pc + 1) * 512], 32, r_stg1c, eng="pool")
        b = next_psf()
        K.op("pe", tr_group([PSF[:, b, q * 32:(q + 1) * 32] for q in range(4)],
                            [STG1C[0:32, q * 128:(q + 1) * 128] for q in range(4)], IDF[0:32, 0:32]),
             reads=[r_stg1c, r_id], writes=[r_psf[b]])
        K.op("act", lambda e, b=b, pc=pc: e.activation(
            out=STF[:, pc * 4:(pc + 1) * 4, :], in_=PSF[:, b, 0:128].rearrange("p (q r) -> p q r", q=4),
            func=AF.Copy), reads=[r_psf[b]], writes=[r_stf])
        rel_psf(b)

    def ring_load(pieces):
        i = next_ring()
        slot = RING[i]

        def fn(e):
            res = []
            for (c0, n, src) in pieces:
                res.append(e.dma_start(out=slot[:, :, c0:c0 + n], in_=src.rearrange("(kc p) n -> p kc n", p=128)))
            return res
        K.dma("pool", fn, s_ring[i], len(pieces), writes=[r_ring[i]])
        return i

    def adaln_vec(col0, kind, GT_out, bcol=None, gcol=None, slot=None):
        i = ring_load([(0, 512, wada_d[:, col0:col0 + 512])]) if slot is None else slot
        b = next_psf()
        K.op("pe", mm_group(PSF[0:18, b, :], [SC[:, kc, :] for kc in range(KC)], [RING[i][:, kc, :] for kc in range(KC)]),
             reads=[r_sc, r_ring[i]], writes=[r_psf[b]])
        K.op("act", lambda e, b=b: e.activation(out=MROW[0:18, :], in_=PSF[0:18, b, :], func=AF.Copy),
             reads=[r_psf[b]], writes=[r_mrow])
        rel_psf(b)
        b2 = next_psf()
        K.op("pe", tr_group([PSF[:, b2, q * 18:(q + 1) * 18] for q in range(4)],
                            [MROW[0:18, q * 128:(q + 1) * 128] for q in range(4)], IDF[0:18, 0:18]),
             reads=[r_mrow, r_id], writes=[r_psf[b2]])
        src = PSF[:, b2, 0:72].rearrange("p (q s) -> p q s", q=4)[:, :, 0:17]
        dst, c0 = GT_out
        if kind == "scale":
            fns = [lambda e: e.tensor_tensor(dst[:, c0:c0 + 4, :], src, VD[:, bcol:bcol + 4].unsqueeze(2).to_broadcast([128, 4, 17]), ALU.add),
                   lambda e: e.tensor_tensor(dst[:, c0:c0 + 4, :], dst[:, c0:c0 + 4, :],
                                             VECS[:, gcol:gcol + 4].unsqueeze(2).to_broadcast([128, 4, 17]), ALU.mult)]
        else:
            fns = [lambda e: e.tensor_tensor(dst[:, c0:c0 + 4, :], src,
                                             VECS[:, bcol:bcol + 4].unsqueeze(2).to_broadcast([128, 4, 17]), ALU.add)]
        chain(K, "dve", fns, [r_psf[b2], r_vd2, r_vecs], [r_mod])
        rel_psf(b2)

    def adaln_gate(col0, row_b, row_g, dstP=None, dstS=None, r_dst=None):
        sem = next_misc()
        gx = next_xt()
        ROWG = XT[gx]
        K.dma("sp", lambda e: [e.dma_start(out=ROWB, in_=rows_d[row_b:row_b + 1, :].to_broadcast([128, D])),
                               e.dma_start(out=ROWG, in_=rows_d[row_g:row_g + 1, :].to_broadcast([128, D]))],
              sem, 2, writes=[r_tmp, r_xt[gx]])
        for hb in range(2):
            i = ring_load([(0, 512, wada_d[:, col0 + hb * 512:col0 + (hb + 1) * 512])])
            for (M, m0, dst) in ((128, 0, dstP), (TS, 128, dstS)):
                b = next_psf()
                K.op("pe", mm_group(PSF[0:M, b, :], [REP[:, kc, m0:m0 + M] for kc in range(KC)],
                                    [RING[i][:, kc, :] for kc in range(KC)]),
                     reads=[r_rep, r_ring[i]], writes=[r_psf[b]])
                sl = slice(hb * 512, (hb + 1) * 512)
                chain(K, "dve", [lambda e, b=b, M=M, dst=dst, sl=sl: e.tensor_tensor(dst[0:M, sl], PSF[0:M, b, :], ROWB[0:M, sl], ALU.add),
                                 lambda e, M=M, dst=dst, sl=sl: e.tensor_tensor(dst[0:M, sl], dst[0:M, sl], ROWG[0:M, sl], ALU.mult)],
                      [r_psf[b], r_tmp, r_xt[gx]], [r_dst])
                rel_psf(b)
        release_xt(gx)

    def tile_rows(t):
        return (t * 128, 128) if t < 16 else (T, TS)

    ST_SS1, ST_RS1, ST_SSM, ST_RSM, ST_SS2, ST_RS2, ST_SSF, ST_RSF = range(8)
    r_stat = [[R("stat%d_%d" % (k, t)) for t in range(17)] for k in range(8)]

    def rms_stats(src_ap, rows, k_ss, k_rs, t, src_regs):
        K.op("act", lambda e: e.activation(out=JUNK[0:rows, :], in_=src_ap, func=AF.Square,
                                           accum_out=STAT[0:rows, k_ss, t:t + 1]),
             reads=src_regs, writes=[r_stat[k_ss][t]])
        K.op("act", lambda e: e.activation(out=STAT[0:rows, k_rs, t:t + 1], in_=STAT[0:rows, k_ss, t:t + 1],
                                           func=AF.Sqrt, bias=EPSB[0:rows, :], scale=1.0 / D),
             reads=[r_stat[k_ss][t], r_vd], writes=[r_stat[k_rs][t]])
        K.op("dve", lambda e: e.reciprocal(STAT[0:rows, k_rs, t:t + 1], STAT[0:rows, k_rs, t:t + 1]),
             reads=[r_stat[k_rs][t]], writes=[r_stat[k_rs][t]])

    def nf_scale(t, xi, k_rs, XNl, r_xnl):
        r0, rows = tile_rows(t)
        j = xn_i[0] % len(XNl)
        xn_i[0] += 1
        K.op("dve", lambda e: e.tensor_scalar(XNl[j][0:rows, :], XT[xi][0:rows, :], STAT[0:rows, k_rs, t:t + 1], None, ALU.mult),
             reads=[r_xt[xi], r_stat[k_rs][t]], writes=[r_xnl[j]])
        return j

    nf_bank = {}

    def nf_transpose(t, j, XNl, r_xnl, GT, STt, dstH, dcol0, r_dst, part="all", n_act=4):
        r0, rows = tile_rows(t)
        if part == "act":
            b = nf_bank.pop(t)
        else:
            b = next_psb()
            K.op("pe", tr_group([PSB[:, b, kc * 128:kc * 128 + rows] for kc in range(KC)],
                                [XNl[j][0:rows, kc * 128:(kc + 1) * 128] for kc in range(KC)], IDB[0:rows, 0:rows]),
                 reads=[r_xnl[j], r_idb], writes=[r_psb[b]])
            if part == "dve":
                nf_bank[t] = b
        src = PSB[:, b, :].rearrange("p (k n) -> p k n", k=KC)
        r_dst_a, r_dst_d = r_dst if isinstance(r_dst, tuple) else (r_dst, r_dst)
        if t < 16:
            def f_act(e):
                inst = None
                for kc in range(0, n_act):
                    inst = e.activation(out=dstH[:, kc, dcol0:dcol0 + 128], in_=src[:, kc, :], func=AF.Identity,
                                        bias=STt[:, kc, 0:1], scale=GT[:, kc, 0:1])
                return inst
            nd = KC - n_act
            if part in ("all", "act"):
                K.op("act", f_act, reads=[r_psb[b], r_mod], writes=[r_dst_a])
            if part in ("all", "dve"):
                chain(K, "dve", [lambda e: e.tensor_tensor(dstH[:, n_act:8, dcol0:dcol0 + 128], src[:, n_act:8, :],
                                                           GT[:, n_act:8, 0:1].to_broadcast([128, nd, 128]), ALU.mult),
                                 lambda e: e.tensor_tensor(dstH[:, n_act:8, dcol0:dcol0 + 128], dstH[:, n_act:8, dcol0:dcol0 + 128],
                                                           STt[:, n_act:8, 0:1].to_broadcast([128, nd, 128]), ALU.add)],
                      [r_psb[b], r_mod], [r_dst_d])
        else:
            o_ = dstH[:, :, dcol0:dcol0 + TS].rearrange("p k (s r) -> p k s r", r=DS)
            i_ = src[:, :, 0:TS].rearrange("p k (s r) -> p k s r", r=DS)
            chain(K, "dve", [lambda e: e.tensor_tensor(o_, i_, GT[:, :, 1:17].unsqueeze(3).to_broadcast([128, KC, NS, DS]), ALU.mult),
                             lambda e: e.tensor_tensor(o_, o_, STt[:, :, 1:17].unsqueeze(3).to_broadcast([128, KC, NS, DS]), ALU.add)],
                  [r_psb[b], r_mod], [r_dst])

    def load_x(t, src_d, reads=(), eng="sp"):
        r0, rows = tile_rows(t)
        xi = next_xt()
        K.dma(eng, lambda e: [e.dma_start(out=XT[xi][0:rows, :], in_=src_d[r0:r0 + rows, :])],
              s_xtp[xi] if eng == "pool" else s_xt[xi], 1, reads=reads, writes=[r_xt[xi]])
        return xi

    state_loads_early()
    adaln_vec(1 * D + 0, "scale", (G1T, 0), bcol=0, gcol=V_GPRE)
    adaln_vec(1 * D + 512, "scale", (G1T, 4), bcol=4, gcol=V_GPRE + 4)
    adaln_vec(0 * D + 0, "shift", (S1T, 0), bcol=V_BADA + 0)
    adaln_vec(0 * D + 512, "shift", (S1T, 4), bcol=V_BADA + 4)
    state_transposes_early()

    r_h = [R("h%d" % t) for t in range(17)]
    p1a_A = {}
    p1a_state = {"a": 0, "b": 0}

    def p1a_stage_a(t):
        r0, rows = tile_rows(t)
        xi = load_x(t, x_d)
        rms_stats(XT[xi][0:rows, :], rows, ST_SS1, ST_RS1, t, [r_xt[xi]])
        p1a_A[t] = nf_scale(t, xi, ST_RS1, XN1, r_xn1)
        release_xt(xi)

    def p1a_stage_b(t):
        r0, rows = tile_rows(t)
        nf_transpose(t, p1a_A[t], XN1, r_xn1, G1T, S1T, AH, r0, r_h[t])

    def p1a_advance(upto_b):
        while p1a_state["b"] < min(upto_b, 17):
            while p1a_state["a"] < 17 and p1a_state["a"] < p1a_state["b"] + NXN1 - 1:
                p1a_stage_a(p1a_state["a"])
                p1a_state["a"] += 1
            p1a_stage_b(p1a_state["b"])
            p1a_state["b"] += 1

    for _ in range(NXN1 - 1):
        p1a_stage_a(p1a_state["a"])
        p1a_state["a"] += 1

    COLT = [(0, 512), (512, 512), (1024, 512), (1536, 512), (T, TS)]
    r_by = [[R("by%d_%d" % (kc, c)) for c in range(5)] for kc in range(KC)]

    def h_regs(c):
        return [r_h[16]] if c == 4 else r_h[4 * c:4 * c + 4]

    r_tb = [R("tb%d" % i) for i in range(NTB)]
    r_db = [R("db%d" % i) for i in range(3)]
    for i_ in (8, 9):
        K.op("pool", lambda e, i_=i_: e.memset(TBUF[i_], 0.0), writes=[r_tb[i_]])

    wi_pool = ring_load([(0, 512, win_d[:, 0:512])])
    SBW = NS * (PB + DS)
    pool_items = [(g, c) for c in range(5) for g in range(4)]
    pool_ctx = {}

    def pool_stage_a(n):
        g, c = pool_items[n]
        w = WIN[g]
        nlev = g + 1
        c0, N = COLT[c]
        samp = (c == 4)
        need = 17 if samp else 4 * (c + 1)
        p1a_advance(min(17, need + g + 2))
        vi = 2 * g + c % 2
        V, rV = TBUF[vi], r_tb[vi]
        Vp, rVp = TBUF[2 * g + 1 - c % 2], r_tb[2 * g + 1 - c % 2]
        b = next_psf()
        K.op("pe", mm_group(PSF[:, b, 0:N], [RING[wi_pool][:, kc, g * 128:(g + 1) * 128] for kc in range(KC)],
                            [AH[:, kc, c0:c0 + N] for kc in range(KC)]),
             reads=[r_ring[wi_pool]] + h_regs(c), writes=[r_psf[b]])
        if not samp:
            Wd = 16 + N

            def f_ev(e, V=V, Vp=Vp, b=b, c=c, N=N):
                if c == 0:
                    e.memzero(V[:, 0:16])
                else:
                    e.activation(out=V[:, 1:16], in_=Vp[:, 513:528], func=AF.Copy)
                return e.activation(out=V[:, 16:16 + N], in_=PSF[:, b, 0:N], func=AF.Copy)
            K.op("act", f_ev, reads=[r_psf[b]] + ([rVp] if c else []), writes=[rV])
            rel_psf(b)
        else:
            Wd = SBW

            def f_ev(e, V=V, b=b, g=g):
                v3 = V[:, 0:SBW].rearrange("p (s r) -> p s r", r=PB + DS)
                e.activation(out=v3[:, :, 0:PB], in_=VS[:, g, :, 0:PB], func=AF.Copy)
                e.activation(out=v3[:, :, PB:PB + DS], in_=PSF[:, b, 0:TS].rearrange("p (s r) -> p s r", r=DS), func=AF.Copy)
                return e.activation(out=SMP[:, g, :], in_=PSF[:, b, 0:TS], func=AF.Copy)
            K.op("act", f_ev, reads=[r_psf[b], r_vs], writes=[rV, r_smp[g]])
            rel_psf(b)
        need_lo = 16 if not samp else 0
        bounds = []
        cur_lo = need_lo
        for lev in range(nlev, 0, -1):
            sh = 1 << (lev - 1)
            bounds.append((lev, sh, max(cur_lo, sh)))
            cur_lo = max(cur_lo - sh, 0)
        bounds.reverse()
        srcb, rsrc = V, rV
        pp = [(TBUF[8], r_tb[8]), (TBUF[9], r_tb[9])]
        for n_, (lev, sh, lo_) in enumerate(bounds):
            dstb, rdst = pp[n_ % 2]
            K.op("dve", lambda e, dstb=dstb, srcb=srcb, lo_=lo_, sh=sh, Wd=Wd: e.tensor_tensor(
                dstb[:, lo_:Wd], srcb[:, lo_:Wd], srcb[:, lo_ - sh:Wd - sh], ALU.add), reads=[rsrc], writes=[rdst])
            srcb, rsrc = dstb, rdst
        Sw, rSw = srcb, rsrc
        di = n % 3
        Dt, rD = DBUF[di], r_db[di]
        if not samp:
            fns = [lambda e, Sw=Sw, V=V, Dt=Dt, w=w, N=N: e.scalar_tensor_tensor(
                Dt[:, 0:N], Sw[:, 16:16 + N], 1.0 / w, V[:, 16:16 + N], ALU.mult, ALU.subtract)]
            if c == 0:
                fns.append(lambda e, Sw=Sw, g=g: e.tensor_tensor(Sw[:, 16:32], Sw[:, 16:32], INVC[:, g, :], ALU.mult))
                fns.append(lambda e, Sw=Sw, V=V, Dt=Dt: e.tensor_tensor(Dt[:, 0:16], Sw[:, 16:32], V[:, 16:32], ALU.subtract))
            chain(K, "dve", fns, [rSw, rV, r_invc], [rD, rSw])
            rhs = Dt[:, 0:N]
        else:
            def f_d(e, Sw=Sw, V=V, Dt=Dt, w=w):
                s3 = Sw[:, 0:SBW].rearrange("p (s r) -> p s r", r=PB + DS)
                v3 = V[:, 0:SBW].rearrange("p (s r) -> p s r", r=PB + DS)
                return e.scalar_tensor_tensor(Dt[:, 0:TS].rearrange("p (s r) -> p s r", r=DS), s3[:, :, PB:PB + DS],
                                              1.0 / w, v3[:, :, PB:PB + DS], ALU.mult, ALU.subtract)
            K.op("dve", f_d, reads=[rSw, rV], writes=[rD])
            rhs = Dt[:, 0:TS]
        pool_ctx[n] = (rhs, rD, V, rV)

    def pool_stage_b(n):
        g, c = pool_items[n]
        c0, N = COLT[c]
        samp = (c == 4)
        rhs, rD, V, rV = pool_ctx.pop(n)
        b2 = next_psf()
        K.op("pe", mm_group(PSF[:, b2, 0:N], [POOLW[:, g, :]], [rhs]), reads=[r_poolw, rD], writes=[r_psf[b2]])
        K.op("act", lambda e, b2=b2, g=g, c0=c0, N=N: e.activation(
            out=BY[:, g, c0:c0 + N], in_=PSF[:, b2, 0:N], func=AF.Copy, scale=VECS[:, V_PSC + g:V_PSC + g + 1]),
            reads=[r_psf[b2], r_vecs], writes=[r_by[g][c]])
        rel_psf(b2)
        if c == 3:
            b3 = next_psf()
            K.op("pe", tr_group([PSF[0:PB, b3, 0:128]], [V[:, 16 + 512 - PB:16 + 512]], IDF),
                 reads=[rV, r_id], writes=[r_psf[b3]])
            K.op("act", lambda e, b3=b3, g=g: e.activation(out=STG[0:PB, g * 128:(g + 1) * 128],
                                                          in_=PSF[0:PB, b3, 0:128], func=AF.Copy),
                 reads=[r_psf[b3]], writes=[r_stg])
            rel_psf(b3)
        if samp:
            b3 = next_psf()
            K.op("pe", tr_group([PSF[0:TS, b3, 0:128]], [SMP[:, g, :]], IDF),
                 reads=[r_smp[g], r_id], writes=[r_psf[b3]])
            K.op("act", lambda e, b3=b3, g=g: e.activation(out=STG2[0:TS, g * 128:(g + 1) * 128],
                                                          in_=PSF[0:TS, b3, 0:128], func=AF.Copy),
                 reads=[r_psf[b3]], writes=[r_stg2])
            rel_psf(b3)

    for n in range(len(pool_items) + 2):
        if n < len(pool_items):
            pool_stage_a(n)
        if n >= 2:
            pool_stage_b(n - 2)
    out_ops.append(K.dma("sp", lambda e: [e.dma_start(out=npp_d, in_=STG[0:PB, :])], s_o_stg, 1, reads=[r_stg]))
    out_ops.append(K.dma("sp", lambda e: [e.dma_start(out=nps_d[s, PB - DS:PB, :], in_=STG2[s * DS:(s + 1) * DS, :])
                                          for s in range(NS)], s_o_stg2, NS, reads=[r_stg2]))
    out_ops.append(K.dma("sp", lambda e: [e.dma_start(
        out=nps_d[:, 0:PB - DS, :], in_=stp_d.rearrange("(s r) f -> s r f", r=PB)[:, DS:PB, :])], s_o_cp, 1))

    if STAGE == 2:
        dump("BY", BY[:, 0:4, :], [r_by[kc][c] for kc in range(4) for c in range(5)])
        return finish()
    adaln_gate(2 * D, 0, 1, GT1P, GT1S, r_gt1)

    ffn_pieces = list(range(11))
    K.dma("pool", lambda e: [e.dma_start(out=WO0, in_=wout_d[:, 0:512].rearrange("(kc p) n -> p kc n", p=128))],
          s_wo0, 1, writes=[r_wo0])
    conv_deferred = []
    ada_cols = [4 * D + 0, 4 * D + 512, 3 * D + 0, 3 * D + 512]
    for j in range(4):
        ada_slot = None
        wi = ring_load([(0, 128, win_d[:, 512 + j * 128:512 + (j + 1) * 128]),
                        (128, 128, win_d[:, 1024 + j * 128:1024 + (j + 1) * 128]),
                        (256, 128, win_d[:, 1536 + j * 128:1536 + (j + 1) * 128])])
        w0 = VECS[:, V_CW + 0 * 4 + j:V_CW + 0 * 4 + j + 1]
        w1 = VECS[:, V_CW + 1 * 4 + j:V_CW + 1 * 4 + j + 1]
        w2 = VECS[:, V_CW + 2 * 4 + j:V_CW + 2 * 4 + j + 1]
        for c in range(5):
            c0, N = COLT[c]
            samp = (c == 4)
            if c == 3:
                ada_slot = ring_load([(0, 512, wada_d[:, ada_cols[j]:ada_cols[j] + 512])])
            if ffn_pieces:
                ffn_state_piece(ffn_pieces.pop(0))
            bx, bc, bb = next_psf(), next_psf(), next_psf()
            for (bk, off) in ((bx, 0), (bc, 256), (bb, 128)):
                K.op("pe", mm_group(PSF[:, bk, 0:N], [RING[wi][:, kc, off:off + 128] for kc in range(KC)],
                                    [AH[:, kc, c0:c0 + N] for kc in range(KC)]),
                     reads=[r_ring[wi]] + h_regs(c), writes=[r_psf[bk]])
            while conv_deferred:
                conv_deferred.pop(0)()
            qi = 4 + (c % 2)
            Q, rQ = TBUF[qi], r_tb[qi]
            Qp, rQp = TBUF[4 + (1 - c % 2)], r_tb[4 + (1 - c % 2)]
            XC, rXC = TBUF[6], r_tb[6]
            ACC, rACC = TBUF[7], r_tb[7]
            K.op("act", lambda e, XC=XC, bx=bx, N=N: e.activation(out=XC[:, 0:N], in_=PSF[:, bx, 0:N], func=AF.Copy),
                 reads=[r_psf[bx]], writes=[rXC])
            if not samp:
                def f_q(e, Q=Q, Qp=Qp, XC=XC, bc=bc, c=c, N=N):
                    if c == 0:
                        e.memset(Q[:, 0:2], 0.0)
                    else:
                        e.tensor_copy(Q[:, 0:2], Qp[:, 512:514])
                    return e.tensor_tensor(Q[:, 2:2 + N], PSF[:, bc, 0:N], XC[:, 0:N], ALU.mult)
                K.op("dve", f_q, reads=[r_psf[bc], rXC] + ([rQp] if c else []), writes=[rQ])
                qv = [Q[:, k:k + N] for k in range(3)]
                accv = ACC[:, 0:N]
                bv = PSF[:, bb, 0:N]
                outv = BY[:, 4 + j, c0:c0 + N]
            else:
                q3 = Q[:, 0:NS * 6].rearrange("p (s r) -> p s r", r=6)

                def f_q(e, q3=q3, XC=XC, bc=bc, j=j):
                    e.tensor_copy(q3[:, :, 0:2], QS[:, j, :, 0:2])
                    return e.tensor_tensor(q3[:, :, 2:6], PSF[:, bc, 0:TS].rearrange("p (s r) -> p s r", r=DS),
                                           XC[:, 0:TS].rearrange("p (s r) -> p s r", r=DS), ALU.mult)
                K.op("dve", f_q, reads=[r_psf[bc], rXC, r_qs], writes=[rQ])
                K.op("dve", lambda e, q3=q3: e.tensor_copy(SMP2.rearrange("p (s r) -> p s r", r=2), q3[:, :, 4:6]),
                     reads=[rQ], writes=[r_smp2])
                qv = [q3[:, :, k:k + DS] for k in range(3)]
                accv = ACC[:, 0:TS].rearrange("p (s r) -> p s r", r=DS)
                bv = PSF[:, bb, 0:TS].rearrange("p (s r) -> p s r", r=DS)
                outv = BY[:, 4 + j, c0:c0 + TS].rearrange("p (s r) -> p s r", r=DS)
            chain(K, "dve", [
                lambda e, qv=qv, accv=accv, w2=w2: e.tensor_scalar(accv, qv[2], w2, None, ALU.mult),
                lambda e, qv=qv, accv=accv, w1=w1: e.scalar_tensor_tensor(accv, qv[1], w1, accv, ALU.mult, ALU.add),
                lambda e, qv=qv, accv=accv, w0=w0: e.scalar_tensor_tensor(accv, qv[0], w0, accv, ALU.mult, ALU.add)],
                [rQ, r_vecs], [rACC])
            K.op("dve", lambda e, accv=accv, bv=bv, outv=outv: e.tensor_tensor(outv, accv, bv, ALU.mult),
                 reads=[rACC, r_psf[bb]], writes=[r_by[4 + j][c]])
            rel_psf(bx, bc, bb)
            if c == 3:
                def _d1(Q=Q, rQ=rQ, j=j):
                    b3 = next_psf()
                    K.op("pe", tr_group([PSF[0:2, b3, 0:128]], [Q[:, 512:514]], IDF), reads=[rQ, r_id], writes=[r_psf[b3]])
                    K.op("act", lambda e, b3=b3, j=j: e.activation(out=STG[0:2, j * 128:(j + 1) * 128],
                                                                  in_=PSF[0:2, b3, 0:128], func=AF.Copy),
                         reads=[r_psf[b3]], writes=[r_stg])
                    rel_psf(b3)
                conv_deferred.append(_d1)
            if samp:
                def _d2(j=j):
                    b3 = next_psf()
                    K.op("pe", tr_group([PSF[0:2 * NS, b3, 0:128]], [SMP2], IDF),
                         reads=[r_smp2, r_id], writes=[r_psf[b3]])
                    K.op("act", lambda e, b3=b3, j=j: e.activation(out=STG2[0:2 * NS, j * 128:(j + 1) * 128],
                                                                  in_=PSF[0:2 * NS, b3, 0:128], func=AF.Copy),
                         reads=[r_psf[b3]], writes=[r_stg2])
                    rel_psf(b3)
                conv_deferred.append(_d2)
        if j == 0:
            adaln_vec(4 * D + 0, "scale", (G2T, 0), bcol=8, gcol=V_GFFN, slot=ada_slot)
        elif j == 1:
            adaln_vec(4 * D + 512, "scale", (G2T, 4), bcol=12, gcol=V_GFFN + 4, slot=ada_slot)
        elif j == 2:
            adaln_vec(3 * D + 0, "shift", (S2T, 0), bcol=V_BADA + 16, slot=ada_slot)
            adaln_gate(5 * D, 2, 3, GTP, GTS, r_gt)
        else:
            adaln_vec(3 * D + 512, "shift", (S2T, 4), bcol=V_BADA + 20, slot=ada_slot)
    while conv_deferred:
        conv_deferred.pop(0)()
    out_ops.append(K.dma("sp", lambda e: [e.dma_start(out=ncp_d, in_=STG[0:2, :])], s_o_stg, 1, reads=[r_stg]))
    out_ops.append(K.dma("sp", lambda e: [e.dma_start(out=ncs_d, in_=STG2[0:2 * NS, :])], s_o_stg2, 1, reads=[r_stg2]))

    if STAGE == 3:
        dump("BY", BY, [r_by[kc][c] for kc in range(KC) for c in range(5)])
        return finish()

    wo1 = ring_load([(0, 512, wout_d[:, 512:1024])])
    WOB = [WO0, RING[wo1]]
    r_wob = [r_wo0, r_ring[wo1]]
    r_yd = [R("yd%d" % t) for t in range(17)]
    GROUPS = [list(range(0, 8)), list(range(8, 17))]
    r_h2 = [R("h2_%d" % t) for t in range(17)]
    for t in range(17):
        r_h2[t].alias(r_h)

    def gcol(t):
        return (t % 8) * 128 if t < 16 else 1024

    tok_ctx = {}
    ffn_wi = [{}, {}]

    def token_stage_a(t, mm_lhs, mm_rhs_half, mm_reads, src_d, final, mm_reads2=None):
        r0, rows = tile_rows(t)
        b = next_psf_pair()
        nk = len(mm_lhs)
        h1 = nk // 2
        for hb in range(2):
            rhs = mm_rhs_half(hb)
            if mm_reads2 is None:
                K.op("pe", mm_group(PSF[0:rows, b + hb, :], mm_lhs, rhs), reads=mm_reads, writes=[r_psf[b + hb]])
            else:
                K.op("pe", mm_part(PSF[0:rows, b + hb, :], mm_lhs[:h1], rhs[:h1], True, False), reads=mm_reads, writes=[r_psf[b + hb]])
                K.op("pe", mm_part(PSF[0:rows, b + hb, :], mm_lhs[h1:], rhs[h1:], False, True), reads=mm_reads2, writes=[r_psf[b + hb]])
        xi = load_x(t, src_d, reads=([r_yd[t]] if final else []), eng="pool")
        tok_ctx[t] = (b, xi)

    def token_stage_b(t, k_ss, k_rs, final):
        r0, rows = tile_rows(t)
        b, xi = tok_ctx[t]
        if final:
            GTt, r_gtt = (GTP if t < 16 else GTS), r_gt
        else:
            GTt, r_gtt = (GT1P if t < 16 else GT1S), r_gt1
        psv = PSF[0:rows, b:b + 2, :].rearrange("p a n -> p (a n)")
        rms_stats(psv, rows, k_ss, k_rs, t, [r_psf[b], r_psf[b + 1]])
        K.op("dve", lambda e: e.scalar_tensor_tensor(TMP[0:rows, :], psv, STAT[0:rows, k_rs, t:t + 1], GTt[0:rows, :],
                                                     ALU.mult, ALU.mult),
             reads=[r_psf[b], r_psf[b + 1], r_stat[k_rs][t], r_gtt], writes=[r_tmp])
        rel_psf(b, b + 1)
        K.op("dve", lambda e: e.tensor_tensor(XT[xi][0:rows, :], TMP[0:rows, :], XT[xi][0:rows, :], ALU.add),
             reads=[r_tmp, r_xt[xi]], writes=[r_xt[xi]])
        o = K.dma("sp", lambda e: [e.dma_start(out=y_d[r0:r0 + rows, :], in_=XT[xi][0:rows, :])],
                  s_o_xt[xi] if final else next_misc(), 1, reads=[r_xt[xi]], writes=[r_yd[t]])
        if final:
            out_ops.append(o)
        return xi

    h2_ctx = {}

    def h2_stage_a(t, xi, XNl, r_xnl):
        r0, rows = tile_rows(t)
        rms_stats(XT[xi][0:rows, :], rows, ST_SS2, ST_RS2, t, [r_xt[xi]])
        h2_ctx[t] = nf_scale(t, xi, ST_RS2, XNl, r_xnl)

    def h2_stage_b(t, XNl, r_xnl, part="all", n_act=4):
        j = h2_ctx[t] if part == "dve" else h2_ctx.pop(t)
        nf_transpose(t, j, XNl, r_xnl, G2T, S2T, H2, gcol(t), r_h2[t], part=part, n_act=n_act)

    def p1c_a(t):
        r0, rows = tile_rows(t)
        c = 4 if t == 16 else t // 4
        token_stage_a(t, [BY[:, kc, r0:r0 + rows] for kc in range(KC)],
                      lambda hb: [WOB[hb][:, kc, :] for kc in range(KC)],
                      [r_by[kc][c] for kc in range(KC)] + r_wob, x_d, False)

    ffn_wi[0][0] = ring_load([(0, 256, wup_d[:, 0:256]), (256, 256, wup_d[:, DFF:DFF + 256])])
    for step in range(17 + 5):
        t = step - 4
        if 0 <= t < 17 and t in GROUPS[0]:
            h2_stage_b(t, XN1, r_xn1, part="dve", n_act=6)
        if step < 17:
            p1c_a(step)
        t = step - 1
        if 0 <= t < 17:
            xi = token_stage_b(t, ST_SSM, ST_RSM, False)
            tok_ctx[t] = (tok_ctx[t][0], xi)
        t = step - 2
        if 0 <= t < 17:
            if t in GROUPS[0]:
                h2_stage_a(t, tok_ctx[t][1], XN1, r_xn1)
            release_xt(tok_ctx[t][1])
        t = step - 4
        if 0 <= t < 17 and t in GROUPS[0]:
            h2_stage_b(t, XN1, r_xn1, part="act", n_act=6)
    while ffn_pieces:
        ffn_state_piece(ffn_pieces.pop(0))

    if STAGE == 4:
        dump("H2", H2, r_h2[0:8])
        return finish()

    p1_regs = ([r_by[kc][c] for kc in range(KC) for c in range(5)] + r_tb + r_db +
               [r_stg, r_stg2, r_stg1c, r_poolw, r_vs, r_qs, r_mrow, r_smp2, r_gt1, r_wo0] + r_smp + r_xn1 + r_h)
    r_g = [[R("g%d_%d" % (i, k)) for k in range(3)] for i in range(NPAIR)]
    r_wdn = R("wdn")
    r_t2 = [R("t2_%d" % i) for i in range(2 * NT2)]
    r_hb = [[R("hb%d_%d" % (ab, p)) for p in range(2)] for ab in range(2)]
    r_hw = [[R("hw%d_%d" % (ab, p)) for p in range(2)] for ab in range(2)]
    r_usb = [R("usb%d" % i) for i in range(2)]
    r_tsb = [R("tsb%d" % i) for i in range(2)]
    r_ub = r_usb + r_tsb + [x for l in r_hb for x in l] + [x for l in r_hw for x in l]
    r_stg3 = R("stg3")
    for rr in [x for l in r_g for x in l] + [r_wdn, r_stg3] + r_ub + r_t2 + r_xn2:
        rr.alias(p1_regs)
    r_halo = [R("halo%d" % ch) for ch in range(NCH)]
    r_halob = [R("halob%d" % ch) for ch in range(NCH)]
    s_wdn = new_sem("wdn")

    r_wdnq = [R("wdn%d" % q) for q in range(4)]
    s_wdnq = [new_sem("wdnq%d" % q) for q in range(4)]
    for rr in r_wdnq:
        rr.alias(p1_regs)

    def load_wdn(q):
        K.dma("pool", lambda e: [e.dma_start(out=WDN[:, q * 6:min(NPAIR, (q + 1) * 6), :],
                                             in_=wdn_d[q * 768:min(DFF, (q + 1) * 768), :].rearrange("(kc p) n -> p kc n", p=128))],
              s_wdnq[q], 1, writes=[r_wdnq[q]])

    GCT = [[(0, 0, 512, False), (1, 512, 512, False)],
           [(0, 1024, 512, False), (1, 1536, 512, False), (2, T, TS, True)]]

    def h2_regs(gi, k):
        if gi == 0:
            return r_h2[4 * k:4 * k + 4]
        return [r_h2[16]] if k == 2 else r_h2[8 + 4 * k:8 + 4 * k + 4]

    def ffn_prefetch(gi, pb):
        wi_of = ffn_wi[gi]
        if gi == 0 and pb in (2, 4, 6, 8):
            load_wdn(pb // 2 - 1)
        for pq in (pb, pb + 1, pb + 2):
            if pq < NPAIR // 2 and pq not in wi_of:
                wi_of[pq] = ring_load([(0, 256, wup_d[:, pq * 256:(pq + 1) * 256]),
                                       (256, 256, wup_d[:, DFF + pq * 256:DFF + (pq + 1) * 256])])

    def ffn_group(gi):
        ptiles = [(k, c0, N) for (k, c0, N, samp) in GCT[gi] if not samp]
        has_samp = any(samp for (_, _, _, samp) in GCT[gi])
        items = []
        for pb in range(NPAIR // 2):
            for ii in range(2):
                for (k, c0, N) in ptiles:
                    items.append((pb, ii, k, c0, N))
        ctx = {}
        wi_of = ffn_wi[gi]

        def prefetch(pb):
            ffn_prefetch(gi, pb)

        def st_a(n):
            pb, ii, k, c0, N = items[n]
            i = pb * 2 + ii
            if (ii, k) == (0, 0):
                prefetch(pb)
            wi = wi_of[pb]
            gc0 = c0 - 1024 * gi
            bnk = [next_psf(), next_psf()]
            for ab in range(2):
                K.op("pe", mm_group(PSF[:, bnk[ab], 0:N],
                                    [RING[wi][:, kc, ab * 256 + ii * 128:ab * 256 + (ii + 1) * 128] for kc in range(KC)],
                                    [H2[:, kc, gc0:gc0 + N] for kc in range(KC)]),
                     reads=[r_ring[wi]] + h2_regs(gi, k), writes=[r_psf[bnk[ab]]])
            hpar = i % 2
            info = []
            pre_ops, main_ops, post_ops = [], [], []
            for ab in range(2):
                ch = ab * NPAIR + i
                Tt, rT = TB2[ab * NT2 + n % NT2], r_t2[ab * NT2 + n % NT2]
                fw = [VECS[:, V_FW + kk * NCH + ch:V_FW + kk * NCH + ch + 1] for kk in range(3)]
                bk = bnk[ab]
                first = (gi == 0 and k == 0)
                from_halo = (gi == 1 and k == 0)
                last = (k == len(ptiles) - 1)
                if first:
                    hsrc, r_hsrc = None, None
                elif from_halo:
                    hsrc, r_hsrc = HALO[:, ch, :], r_halo[ch]
                else:
                    hsrc, r_hsrc = HBUF[:, ab, hpar, :], r_hb[ab][hpar]
                if last and gi == 0:
                    hdst, r_hdst = HALO[:, ch, :], r_halo[ch]
                elif last:
                    hdst, r_hdst = HALOB[:, ch, :], r_halob[ch]
                else:
                    hdst, r_hdst = HBUF[:, ab, hpar, :], r_hb[ab][hpar]
                tv = Tt[:, 0:N]
                hw1, r_hw1 = HW1[:, ab, hpar, 0:1], r_hw[ab][hpar]
                if from_halo:
                    pre_ops.append((lambda e, hw1=hw1, ch=ch, fw=fw: e.activation(out=hw1, in_=HALO[:, ch, 1:2], func=AF.Copy, scale=fw[1]),
                                    [r_halo[ch], r_vecs], [r_hw1]))

                def f_t0(e, bk=bk, N=N, tv=tv, fw=fw, hdst=hdst, use_h=(hsrc is not None), hw1=hw1):
                    e.activation(out=hdst, in_=PSF[:, bk, N - 2:N], func=AF.Copy)
                    if use_h:
                        e.activation(out=tv[:, 0:1], in_=PSF[:, bk, 0:1], func=AF.Identity, scale=fw[2], bias=hw1)
                        return e.activation(out=tv[:, 1:N], in_=PSF[:, bk, 1:N], func=AF.Copy, scale=fw[2])
                    return e.activation(out=tv, in_=PSF[:, bk, 0:N], func=AF.Copy, scale=fw[2])
                main_ops.append((f_t0, [r_psf[bk], r_vecs] + ([r_hw1] if hsrc is not None else []), [rT, r_hdst]))
                if not last:
                    post_ops.append((lambda e, hw1=hw1, hdst=hdst, fw=fw: e.activation(out=hw1, in_=hdst[:, 1:2], func=AF.Copy, scale=fw[1]),
                                     [r_hdst, r_vecs], [r_hw1]))
                info.append((bk, tv, fw, rT, hsrc, r_hsrc, N))
            for (f_, rd_, wr_) in pre_ops + main_ops + post_ops:
                K.op("act", f_, reads=rd_, writes=wr_)
            ctx[n] = info

        def st_b(n):
            per = []
            for (bk, tv, fw, rT, hsrc, r_hsrc, N) in ctx[n]:
                fns = [lambda e, bk=bk, tv=tv, fw=fw, N=N: e.scalar_tensor_tensor(
                           tv[:, 1:N], PSF[:, bk, 0:N - 1], fw[1], tv[:, 1:N], ALU.mult, ALU.add),
                       lambda e, bk=bk, tv=tv, fw=fw, N=N: e.scalar_tensor_tensor(
                           tv[:, 2:N], PSF[:, bk, 0:N - 2], fw[0], tv[:, 2:N], ALU.mult, ALU.add)]
                rd = [r_psf[bk], r_vecs]
                if hsrc is not None:
                    fns.append(lambda e, tv=tv, fw=fw, hsrc=hsrc: e.scalar_tensor_tensor(
                        tv[:, 0:2], hsrc[:, 0:2], fw[0], tv[:, 0:2], ALU.mult, ALU.add))
                    rd.append(r_hsrc)
                per.append((fns, rd, rT, bk))
            for j in range(max(len(p[0]) for p in per)):
                for (fns, rd, rT, bk) in per:
                    if j < len(fns):
                        K.op("dve", fns[j], reads=rd, writes=[rT])
            for (fns, rd, rT, bk) in per:
                rel_psf(bk)
            ta, rTa = ctx[n][0][1], ctx[n][0][3]
            K.op("act", lambda e, ta=ta: e.activation(out=ta, in_=ta, func=AF.Silu), reads=[rTa], writes=[rTa])

        def st_c(n):
            pb, ii, k, c0, N = items[n]
            i = pb * 2 + ii
            gc0 = c0 - 1024 * gi
            (bka, ta, fwa, rTa, _, _, _), (bkb, tb, fwb, rTb, _, _, _) = ctx.pop(n)
            gv = GT_[:, i, gc0:gc0 + N]
            K.op("pool", lambda e, gv=gv, ta=ta, tb=tb: e.tensor_tensor(gv, ta, tb, ALU.mult),
                 reads=[rTa, rTb], writes=[r_g[i][k]])

        def samp_mm(pb):
            wi = wi_of[pb]
            us, rus = USB[pb % 2], r_usb[pb % 2]
            K.op("pool", lambda e, us=us, pb=pb: e.tensor_copy(
                us[:, 0:2, :, 0:2], STF[:, 2 * pb:2 * pb + 2, :].rearrange("p c (s r) -> p c s r", r=2)),
                reads=[r_stf], writes=[rus])
            K.op("pool", lambda e, us=us, pb=pb: e.tensor_copy(
                us[:, 2:4, :, 0:2], STF[:, NPAIR + 2 * pb:NPAIR + 2 * pb + 2, :].rearrange("p c (s r) -> p c s r", r=2)),
                reads=[r_stf], writes=[rus])
            psx = PSB[:, 0, :].bitcast(F32)
            for q in range(4):
                ab, ii = q // 2, q % 2
                K.op("pe", mm_group(psx[:, q * TS:(q + 1) * TS],
                                    [RING[wi][:, kc, ab * 256 + ii * 128:ab * 256 + (ii + 1) * 128] for kc in range(KC)],
                                    [H2[:, kc, 1024:1024 + TS] for kc in range(KC)]),
                     reads=[r_ring[wi], r_h2[16]], writes=[r_psb[0]])
            K.op("act", lambda e, us=us, psx=psx: e.activation(
                out=us[:, :, :, 2:6], in_=psx[:, 0:4 * TS].rearrange("p (q s r) -> p q s r", q=4, r=DS), func=AF.Copy),
                reads=[r_psb[0]], writes=[rus])

        def samp_post(pb):
            us, rus = USB[pb % 2], r_usb[pb % 2]
            ts, rts = TSB[pb % 2], r_tsb[pb % 2]
            t4 = ts.rearrange("p c (s r) -> p c s r", r=DS)
            fns = []
            for ab in range(2):
                cs = slice(2 * ab, 2 * ab + 2)
                c0_ = ab * NPAIR + 2 * pb

                def wv(kk, c0_=c0_):
                    return VECS[:, V_FW + kk * NCH + c0_:V_FW + kk * NCH + c0_ + 2].unsqueeze(2).unsqueeze(3).to_broadcast([128, 2, NS, DS])
                fns.append(lambda e, cs=cs, wv=wv: e.tensor_tensor(t4[:, cs], us[:, cs, :, 2:6], wv(2), ALU.mult))
            chain(K, "pool", fns, [rus, r_vecs], [rts])
            tmp, rtmp = TSB[1 - pb % 2], r_tsb[1 - pb % 2]
            tm4 = tmp.rearrange("p c (s r) -> p c s r", r=DS)
            for kk in (1, 0):
                for ab in range(2):
                    cs = slice(2 * ab, 2 * ab + 2)
                    c0_ = ab * NPAIR + 2 * pb
                    wvk = VECS[:, V_FW + kk * NCH + c0_:V_FW + kk * NCH + c0_ + 2].unsqueeze(2).unsqueeze(3).to_broadcast([128, 2, NS, DS])
                    K.op("pool", lambda e, cs=cs, wvk=wvk, kk=kk: e.tensor_tensor(tm4[:, cs], us[:, cs, :, kk:kk + DS], wvk, ALU.mult),
                         reads=[rus, r_vecs], writes=[rtmp])
                    K.op("pool", lambda e, cs=cs: e.tensor_tensor(t4[:, cs], t4[:, cs], tm4[:, cs], ALU.add),
                         reads=[rtmp, rts], writes=[rts])
            K.op("act", lambda e: e.activation(out=ts[:, 0:2, :], in_=ts[:, 0:2, :], func=AF.Silu), reads=[rts], writes=[rts])
            K.op("pool", lambda e, pb=pb: e.tensor_tensor(GT_[:, 2 * pb:2 * pb + 2, 1024:1024 + TS], ts[:, 0:2, :], ts[:, 2:4, :], ALU.mult),
                 reads=[rts], writes=[r_g[2 * pb][2], r_g[2 * pb + 1][2]])
            K.op("pool", lambda e, us=us, pb=pb: e.tensor_copy(
                STF[:, 2 * pb:2 * pb + 2, :].rearrange("p c (s r) -> p c s r", r=2), us[:, 0:2, :, 4:6]),
                reads=[rus], writes=[r_stf])
            K.op("pool", lambda e, us=us, pb=pb: e.tensor_copy(
                STF[:, NPAIR + 2 * pb:NPAIR + 2 * pb + 2, :].rearrange("p c (s r) -> p c s r", r=2), us[:, 2:4, :, 4:6]),
                reads=[rus], writes=[r_stf])

        NI = len(items)
        per_pb = 2 * len(ptiles)
        for step in range(NI + 2):
            if 0 <= step - 2 < NI:
                st_c(step - 2)
            if step < NI:
                st_a(step)
                if has_samp and step % per_pb == per_pb - 1:
                    samp_mm(step // per_pb)
            if 0 <= step - 1 < NI:
                st_b(step - 1)
                if has_samp and (step - 1) % per_pb == per_pb - 1:
                    samp_post((step - 1) // per_pb)

    def down_a(t):
        r0, rows = tile_rows(t)
        k = 2 if t == 16 else (t % 8) // 4
        g0 = gcol(t)
        token_stage_a(t, [GT_[:, i, g0:g0 + rows] for i in range(NPAIR)],
                      lambda hb: [WDN[:, i, hb * 512:(hb + 1) * 512] for i in range(NPAIR)],
                      [r_g[i][k] for i in range(NPAIR // 2)] + r_wdnq[0:2], y_d, True,
                      mm_reads2=[r_g[i][k] for i in range(NPAIR // 2, NPAIR)] + r_wdnq[2:4])

    ffn_group(0)
    if STAGE == 5:
        dump("G", GT_, [r_g[i][k] for i in range(NPAIR) for k in range(2)])
        return finish()

    lazy = list(GROUPS[1])
    lazy_ctx = {}

    def lazy_a(tt):
        if tt < 16:
            r_h2[tt].alias([r_h2[tt - 8]])
        xi = load_x(tt, y_d, reads=[r_yd[tt]], eng="pool")
        h2_stage_a(tt, xi, XN2, r_xn2)
        release_xt(xi)

    def lazy_b(tt):
        h2_stage_b(tt, XN2, r_xn2)

    g0t = GROUPS[0]
    la = 0
    lb = 0
    for step in range(len(g0t) + 1):
        if step == 3:
            ffn_wi[1][0] = ring_load([(0, 256, wup_d[:, 0:256]), (256, 256, wup_d[:, DFF:DFF + 256])])
            ffn_wi[1][1] = ring_load([(0, 256, wup_d[:, 256:512]), (256, 256, wup_d[:, DFF + 256:DFF + 512])])
        if step < len(g0t):
            down_a(g0t[step])
        for _ in range(2):
            if la < len(lazy) and la < lb + 2:
                lazy_a(lazy[la])
                la += 1
        if step >= 1:
            release_xt(token_stage_b(g0t[step - 1], ST_SSF, ST_RSF, True))
        for _ in range(2):
            if step >= 1 and lb < la and lb < len(lazy) and (lb < la - 1 or la == len(lazy)):
                lazy_b(lazy[lb])
                lb += 1
    while lb < len(lazy):
        if la < len(lazy):
            lazy_a(lazy[la])
            la += 1
        lazy_b(lazy[lb])
        lb += 1

    ffn_group(1)

    def ffn_state_out(pc):
        b = next_psf()
        K.op("pe", tr_group([PSF[0:2, b, q * 128:(q + 1) * 128] for q in range(4)],
                            [HALOB[:, _ch(pc, q), :] for q in range(4)], IDF),
             reads=[r_halob[_ch(pc, q)] for q in range(4)] + [r_id], writes=[r_psf[b]])
        b2 = next_psf()
        K.op("pe", tr_group([PSF[0:2 * NS, b2, q * 128:(q + 1) * 128] for q in range(4)],
                            [STF[:, _ch(pc, q), :] for q in range(4)], IDF), reads=[r_stf, r_id], writes=[r_psf[b2]])
        K.op("act", lambda e, b=b: e.activation(out=STG3[0:2, :], in_=PSF[0:2, b, :], func=AF.Copy),
             reads=[r_psf[b]], writes=[r_stg3])
        rel_psf(b)
        out_ops.append(K.dma("sp", lambda e, pc=pc: [e.dma_start(out=nfp_d[:, pc * 512:(pc + 1) * 512], in_=STG3[0:2, :])],
                             s_o_stg3, 1, reads=[r_stg3]))
        r_t = r_t2[pc % 2]
        tb = TB2[pc % 2]
        K.op("act", lambda e, b2=b2, tb=tb: e.activation(out=tb[0:2 * NS, 0:512], in_=PSF[0:2 * NS, b2, :], func=AF.Copy),
             reads=[r_psf[b2]], writes=[r_t])
        rel_psf(b2)
        out_ops.append(K.dma("sp", lambda e, pc=pc, tb=tb: [e.dma_start(out=nfs_d[:, pc * 512:(pc + 1) * 512],
                                                                        in_=tb[0:2 * NS, 0:512])],
                             s_o_tb[pc % 2], 1, reads=[r_t]))

    g1t = GROUPS[1]
    so_pieces = list(range(11))
    for step in range(len(g1t) + 1):
        if step < len(g1t):
            down_a(g1t[step])
        if step >= 1:
            release_xt(token_stage_b(g1t[step - 1], ST_SSF, ST_RSF, True))
        if step == 1:
            while so_pieces:
                ffn_state_out(so_pieces.pop(0))

    return finish()


def _ch(pc, q):
    return pc * 4 + q


_NC_CACHE = {}


def _fm(v, n):
    return np.ascontiguousarray(np.asarray(v, np.float32).reshape(n, 128).T)


def kernel(x_prompt, x_sample, state_pool, state_conv, state_ffn, c_prompt, c_sample,
           w_ada, b_ada, g_pre_mix, g_post_mix, g_pre_ffn, g_post_ffn, w_in, pool_w,
           pool_scale, conv_w, w_out, ffn_w_up, ffn_conv_w, ffn_w_down):
    f = lambda a: np.ascontiguousarray(np.asarray(a, dtype=np.float32))
    x_prompt, x_sample = f(x_prompt), f(x_sample)
    state_pool, state_conv, state_ffn = f(state_pool)[0], f(state_conv)[0], f(state_ffn)[0]
    c_prompt, c_sample = f(c_prompt), f(c_sample)
    b_ada_ = f(b_ada)[0]
    conv_w_ = f(conv_w)[0]
    fcw = f(ffn_conv_w)[0]
    vecs = np.concatenate([
        _fm(f(g_pre_mix)[0], 8), _fm(f(g_pre_ffn)[0], 8),
        _fm(b_ada_[0:D], 8), _fm(b_ada_[D:2 * D], 8), _fm(b_ada_[3 * D:4 * D], 8), _fm(b_ada_[4 * D:5 * D], 8),
        _fm(f(pool_scale)[0], 4),
        _fm(conv_w_[0], 4), _fm(conv_w_[1], 4), _fm(conv_w_[2], 4),
        _fm(fcw[0], NCH), _fm(fcw[1], NCH), _fm(fcw[2], NCH)], axis=1)
    assert vecs.shape == (128, NV)
    rows = np.stack([b_ada_[2 * D:3 * D], f(g_post_mix)[0], b_ada_[5 * D:6 * D], f(g_post_ffn)[0]], axis=0)
    shared = {
        "w_ada": f(w_ada)[0], "w_in": f(w_in)[0], "pool_w": f(pool_w)[0].reshape(512, 128),
        "w_out": f(w_out)[0], "w_up": f(ffn_w_up)[0], "w_down": f(ffn_w_down)[0],
        "vecs": np.ascontiguousarray(vecs), "rows": np.ascontiguousarray(rows),
        "ident": np.eye(128, dtype=np.float32),
    }
    in_maps = []
    for i in range(NCORES):
        sl = slice(i * NS, (i + 1) * NS)
        m = dict(shared)
        m["x"] = np.ascontiguousarray(np.concatenate([x_prompt[i], x_sample[sl].reshape(TS, D)], axis=0))
        m["cT"] = np.ascontiguousarray(np.concatenate([c_prompt[i:i + 1], c_sample[sl], np.zeros((1, D), np.float32)], axis=0).T)
        m["st_pool"] = np.ascontiguousarray(state_pool[sl].reshape(NS * PB, 512))
        m["st_conv"] = np.ascontiguousarray(state_conv[sl].reshape(NS * 2, 512))
        m["st_ffn"] = np.ascontiguousarray(state_ffn[sl].reshape(NS * 2, 2 * DFF))
        in_maps.append(m)
    if "nc" not in _NC_CACHE:
        _NC_CACHE["nc"] = build_program()
    nc = _NC_CACHE["nc"]
    res = run_bass_kernel_spmd(nc, in_maps, core_ids=list(range(NCORES)))
    rs = res.results
    y_p = np.stack([rs[i]["y"][0:T] for i in range(NCORES)], axis=0)
    y_s = np.concatenate([rs[i]["y"][T:NTOK].reshape(NS, DS, D) for i in range(NCORES)], axis=0)
    npp = np.stack([rs[i]["npp"] for i in range(NCORES)], axis=0)[None]
    ncp = np.stack([rs[i]["ncp"] for i in range(NCORES)], axis=0)[None]
    nfp = np.stack([rs[i]["nfp"] for i in range(NCORES)], axis=0)[None]
    nps = np.concatenate([rs[i]["nps"] for i in range(NCORES)], axis=0)[None]
    ncs = np.concatenate([rs[i]["ncs"].reshape(NS, 2, 512) for i in range(NCORES)], axis=0)[None]
    nfs = np.concatenate([rs[i]["nfs"].reshape(NS, 2, 2 * DFF) for i in range(NCORES)], axis=0)[None]
    return tuple(np.ascontiguousarray(a.astype(np.float32)) for a in (y_p, y_s, npp, ncp, nfp, nps, ncs, nfs))
```
